# Optimizing a Trainium2 kernel written in Bass

```python
import jax, jax.numpy as jnp
from jax import lax
import numpy as np

D_MODEL = 1024
BATCH = 4
SEQ = 8192
DEPTH = 1

N_MEM = 256
CONV_CH = D_MODEL
CONV_WIDTH = 31
N_HEADS = 8
HEAD_DIM = D_MODEL // N_HEADS
ATTN_W = N_HEADS * HEAD_DIM
IDX_HEADS = 8
IDX_DIM = 64
TOPK_MAX = 256
Q_BLOCK = 128
X_HEADS = 4
X_HEAD_DIM = 128
X_W = X_HEADS * X_HEAD_DIM
D_FF = 4 * D_MODEL
EPS = 1e-6

IN_SIZES = (CONV_CH, CONV_CH, ATTN_W, ATTN_W, ATTN_W, IDX_HEADS * IDX_DIM, IDX_DIM, IDX_HEADS, D_MODEL, D_MODEL)
IN_WIDTH = sum(IN_SIZES)

kernel_name = "hybrid_conformer_dsa_gated_block"


def _rmsnorm(x, g):
    x32 = x.astype(jnp.float32)
    y = x32 * lax.rsqrt(jnp.mean(x32 * x32, axis=-1, keepdims=True) + EPS)
    return y.astype(x.dtype) * g


def _layernorm(x, g, b):
    x32 = x.astype(jnp.float32)
    mu = jnp.mean(x32, axis=-1, keepdims=True)
    xc = x32 - mu
    var = jnp.mean(xc * xc, axis=-1, keepdims=True)
    return (xc * lax.rsqrt(var + EPS)).astype(x.dtype) * g + b


def _split_cols(z, sizes):
    offs = []
    acc = 0
    for s in sizes[:-1]:
        acc += s
        offs.append(acc)
    return jnp.split(z, offs, axis=-1)


def _conformer_conv(a, gate, conv_w, conv_b, ln_g, ln_b, w_out):
    u = a * jax.nn.sigmoid(gate)
    u = lax.conv_general_dilated(
        u, conv_w[:, None, :].astype(u.dtype), window_strides=(1,),
        padding=[(CONV_WIDTH - 1, 0)], dimension_numbers=('NWC', 'WIO', 'NWC'),
        feature_group_count=CONV_CH) + conv_b
    u = jax.nn.silu(_layernorm(u, ln_g, ln_b))
    return u @ w_out


def _dsa_attention(q, k, v, qi, ki, wi):
    B, L = q.shape[0], q.shape[1]
    k_top = min(TOPK_MAX, L // 4)
    nb = L // Q_BLOCK
    key_pos = jnp.arange(L)
    ki32 = ki.astype(jnp.float32)

    def to_blocks(a):
        return a.reshape((B, nb, Q_BLOCK) + a.shape[2:]).swapaxes(0, 1)

    def block(args):
        qb, qib, wb, bi = args
        q_pos = bi * Q_BLOCK + jnp.arange(Q_BLOCK)
        causal = key_pos[None, :] <= q_pos[:, None]
        logits = jnp.einsum('bqhd,bsd->bqhs', qib.astype(jnp.float32), ki32) * (IDX_DIM ** -0.5)
        score = jnp.einsum('bqh,bqhs->bqs', wb.astype(jnp.float32), jax.nn.relu(logits))
        score = jnp.where(causal[None], score, -jnp.inf)
        _, idx = lax.top_k(score, k_top)
        valid = idx <= q_pos[None, :, None]
        k_sel = jax.vmap(lambda kb, ib: kb[ib])(k, idx)
        v_sel = jax.vmap(lambda vb, ib: vb[ib])(v, idx)
        s = jnp.einsum('bqhd,bqkhd->bhqk', qb.astype(jnp.float32), k_sel.astype(jnp.float32)) * (HEAD_DIM ** -0.5)
        s = jnp.where(valid[:, None], s, -jnp.inf)
        p = jax.nn.softmax(s, axis=-1)
        return jnp.einsum('bhqk,bqkhd->bqhd', p.astype(v.dtype), v_sel)

    out = lax.map(block, (to_blocks(q), to_blocks(qi), to_blocks(wi), jnp.arange(nb)))
    return out.swapaxes(0, 1).reshape(B, L, ATTN_W)


def _cross_attention(hn, memn, wq, wkv, wo):
    B, L, _ = hn.shape
    q = (hn @ wq).reshape(B, L, X_HEADS, X_HEAD_DIM)
    kv = memn @ wkv
    k, v = jnp.split(kv, 2, axis=-1)
    k = k.reshape(B, N_MEM, X_HEADS, X_HEAD_DIM)
    v = v.reshape(B, N_MEM, X_HEADS, X_HEAD_DIM)
    s = jnp.einsum('bshd,bmhd->bhsm', q.astype(jnp.float32), k.astype(jnp.float32)) * (X_HEAD_DIM ** -0.5)
    p = jax.nn.softmax(s, axis=-1)
    o = jnp.einsum('bhsm,bmhd->bshd', p.astype(v.dtype), v).reshape(B, L, X_W)
    return o @ wo


def setup_inputs(seed: int = 0) -> dict:
    key = jax.random.key(seed)
    ks = jax.random.split(key, 24)
    f32 = jnp.float32

    def w(k, shape, fan_in):
        return jax.random.normal(k, shape, f32) * (fan_in ** -0.5)

    def gain(k, shape):
        return 1.0 + 0.02 * jax.random.normal(k, shape, f32)

    def bias(k, shape):
        return 0.02 * jax.random.normal(k, shape, f32)

    return {
        "x": jax.random.normal(ks[0], (BATCH, SEQ, D_MODEL), f32),
        "mem": jax.random.normal(ks[1], (BATCH, N_MEM, D_MODEL), f32),
        "norm_mix_g": gain(ks[2], (DEPTH, D_MODEL)),
        "w_in": w(ks[3], (DEPTH, D_MODEL, IN_WIDTH), D_MODEL),
        "b_gate": bias(ks[4], (DEPTH, 2 * D_MODEL)),
        "conv_w": w(ks[5], (DEPTH, CONV_WIDTH, CONV_CH), CONV_WIDTH),
        "conv_b": bias(ks[6], (DEPTH, CONV_CH)),
        "conv_ln_g": gain(ks[7], (DEPTH, CONV_CH)),
        "conv_ln_b": bias(ks[8], (DEPTH, CONV_CH)),
        "w_conv_out": w(ks[9], (DEPTH, CONV_CH, D_MODEL), CONV_CH),
        "w_attn_out": w(ks[10], (DEPTH, ATTN_W, D_MODEL), ATTN_W),
        "w_mix_out": w(ks[11], (DEPTH, D_MODEL, D_MODEL), D_MODEL),
        "norm_x_g": gain(ks[12], (DEPTH, D_MODEL)),
        "norm_mem_g": gain(ks[13], (DEPTH, D_MODEL)),
        "wx_q": w(ks[14], (DEPTH, D_MODEL, X_W), D_MODEL),
        "wx_kv": w(ks[15], (DEPTH, D_MODEL, 2 * X_W), D_MODEL),
        "wx_o": w(ks[16], (DEPTH, X_W, D_MODEL), X_W),
        "norm_ffn_g": gain(ks[17], (DEPTH, D_MODEL)),
        "w_ff1": w(ks[18], (DEPTH, D_MODEL, D_FF), D_MODEL),
        "w_ff2": w(ks[19], (DEPTH, D_FF, D_MODEL), D_FF),
        "norm_final_g": gain(ks[20], (D_MODEL,)),
    }


def reference(x, mem, norm_mix_g, w_in, b_gate, conv_w, conv_b, conv_ln_g, conv_ln_b,
              w_conv_out, w_attn_out, w_mix_out, norm_x_g, norm_mem_g, wx_q, wx_kv, wx_o,
              norm_ffn_g, w_ff1, w_ff2, norm_final_g):
    B, L, _ = x.shape
    h = x
    for l in range(DEPTH):
        n = _rmsnorm(h, norm_mix_g[l])
        z = n @ w_in[l]
        c_a, c_g, q, k, v, qi, ki, wi, gc_logit, ga_logit = _split_cols(z, IN_SIZES)
        bg_conv, bg_attn = jnp.split(b_gate[l], 2)
        g_conv = jax.nn.sigmoid(gc_logit + bg_conv)
        g_attn = jax.nn.sigmoid(ga_logit + bg_attn)
        y_conv = _conformer_conv(c_a, c_g, conv_w[l], conv_b[l], conv_ln_g[l], conv_ln_b[l], w_conv_out[l])
        qh = q.reshape(B, L, N_HEADS, HEAD_DIM)
        kh = k.reshape(B, L, N_HEADS, HEAD_DIM)
        vh = v.reshape(B, L, N_HEADS, HEAD_DIM)
        qih = qi.reshape(B, L, IDX_HEADS, IDX_DIM)
        wih = wi * (IDX_HEADS ** -0.5)
        y_attn = _dsa_attention(qh, kh, vh, qih, ki, wih) @ w_attn_out[l]
        h = h + (g_conv * y_conv + g_attn * y_attn) @ w_mix_out[l]
        h = h + _cross_attention(_rmsnorm(h, norm_x_g[l]), _rmsnorm(mem, norm_mem_g[l]),
                                 wx_q[l], wx_kv[l], wx_o[l])
        hf = _rmsnorm(h, norm_ffn_g[l]) @ w_ff1[l]
        h = h + jnp.square(jax.nn.relu(hf)) @ w_ff2[l]
    return _rmsnorm(h, norm_final_g)
```

```python
import contextlib
import numpy as np
import concourse.bass as bass
import concourse.mybir as mybir
from concourse.bass_utils import run_bass_kernel_spmd

F32 = mybir.dt.float32
BF16 = mybir.dt.bfloat16
FP8 = mybir.dt.float8e4
AF = mybir.ActivationFunctionType
ALU = mybir.AluOpType
AX = mybir.AxisListType

NIT = 22
KTOP = 256
EPS = 1e-6
NEG = -1.0e30

U_CA, U_CG, U_Q, U_QI, U_GC, U_GA, U_CO, U_AO, U_MIX, U_XQ, U_XO, U_FF1, U_FF2 = 0, 2, 4, 6, 7, 9, 11, 13, 15, 17, 18, 20, 28
NUNITS = 36
V_GMIX, V_GX, V_GMEM, V_GFFN, V_BGC, V_BGA, V_CB, V_LG, V_LB = 0, 8, 16, 24, 32, 40, 48, 56, 64


class Buf:
    __slots__ = ("name", "w", "r", "dsem", "dcnt")

    def __init__(self, name):
        self.name = name
        self.w = {}
        self.r = {}
        self.dsem = None
        self.dcnt = 0


class Eng:
    def __init__(self, name, sem, selfwait):
        self.name = name
        self.sem = sem
        self.count = 0
        self.seen = {}
        self.ops = []
        self.selfwait = selfwait


def _merge(d, s):
    for k, (sem, val) in s.items():
        if k not in d or d[k][1] < val:
            d[k] = (sem, val)


class KB:
    def __init__(self, nc, stack):
        self.nc = nc
        self.stack = stack
        self.nsem = 0
        self.PE = Eng("pe", self.new_sem(), False)
        self.ACT = Eng("act", self.new_sem(), True)
        self.DVE = Eng("dve", self.new_sem(), True)
        self.POOL = Eng("pool", self.new_sem(), True)
        self.SP = Eng("sp", self.new_sem(), False)
        self.engs = [self.PE, self.ACT, self.DVE, self.POOL, self.SP]
        self.dbufs = []

    def new_sem(self):
        self.nsem += 1
        return self.stack.enter_context(self.nc.semaphore("s%d" % self.nsem))

    def _wait(self, E, deps):
        for key, (sem, val) in deps.items():
            if sem is E.sem and not E.selfwait:
                continue
            if E.seen.get(key, 0) >= val:
                continue
            E.seen[key] = val
            E.ops.append(("wait", sem, val))

    def _deps(self, reads, writes):
        deps = {}
        for b in reads:
            _merge(deps, b.w)
        for b in writes:
            _merge(deps, b.w)
            _merge(deps, b.r)
        return deps

    def _commit(self, tk, reads, writes):
        key = id(tk[0])
        for b in reads:
            if key not in b.r or b.r[key][1] < tk[1]:
                b.r[key] = tk
        for b in writes:
            b.w = {key: tk}
            b.r = {}

    def group(self, E, insts, reads=(), writes=()):
        self._wait(E, self._deps(reads, writes))
        E.count += 1
        tk = (E.sem, E.count)
        n = len(insts)
        for i, (m, a, k) in enumerate(insts):
            E.ops.append(("op", m, a, k, E.sem if i == n - 1 else None, 1))
        self._commit(tk, reads, writes)

    def op(self, E, method, args, kwargs=None, reads=(), writes=()):
        self.group(E, [(method, args, kwargs or {})], reads, writes)

    def dma(self, out, in_, sb, reads=(), writes=(), Q=None):
        Q = Q or self.SP
        deps = self._deps(reads, writes)
        if sb.dsem is None:
            sb.dsem = self.new_sem()
            self.dbufs.append(sb)
        if sb.dcnt > 0:
            _merge(deps, {id(sb.dsem): (sb.dsem, sb.dcnt * 16)})
        self._wait(Q, deps)
        sb.dcnt += 1
        tk = (sb.dsem, sb.dcnt * 16)
        Q.ops.append(("op", "dma_start", (), dict(out=out, in_=in_), sb.dsem, 16))
        self._commit(tk, reads, writes)

    def switch(self, old, new):
        deps = {}
        for b in old:
            _merge(deps, b.w)
            _merge(deps, b.r)
        for b in new:
            d = dict(deps)
            _merge(d, b.w)
            _merge(d, b.r)
            b.w = d
            b.r = {}

    def barrier(self):
        deps = {}
        for E in self.engs:
            if E.count > 0:
                deps[id(E.sem)] = (E.sem, E.count)
        for b in self.dbufs:
            deps[id(b.dsem)] = (b.dsem, b.dcnt * 16)
        for E in self.engs:
            self._wait(E, deps)

    def replay(self, E, e):
        for o in E.ops:
            if o[0] == "wait":
                e.wait_ge(o[1], o[2])
            else:
                _, m, a, k, sem, inc = o
                ins = getattr(e, m)(*a, **k)
                if sem is not None:
                    ins.then_inc(sem, inc)


def build(NG):
    L = 1024 * NG
    NCH = 2 * NG
    nc = bass.Bass("TRN2", target_bir_lowering=False)

    def din(name, shape, dt=F32):
        return nc.dram_tensor(name, list(shape), dt, kind="ExternalInput").ap()

    x_seq = din("x_seq", [L, 1024])
    x_own = din("x_own", [NG, 544, 1024])
    mem = din("mem", [256, 1024])
    cmask = din("cmask", [4, 128, 1024])
    vecs_d = din("vecs", [128, 72])
    cw_d = din("cwfm", [128, 248])
    gfin_d = din("gfin", [1024])
    ident_d = din("ident", [128, 128])
    pow2_d = din("pow2", [128, NIT])
    w_in = din("w_in", [1024, 7752])
    w_co = din("w_conv_out", [1024, 1024])
    w_ao = din("w_attn_out", [1024, 1024])
    w_mix = din("w_mix_out", [1024, 1024])
    wx_q = din("wx_q", [1024, 512])
    wx_kv = din("wx_kv", [1024, 1024])
    wx_o = din("wx_o", [512, 1024])
    w_ff1 = din("w_ff1", [1024, 4096])
    w_ff2 = din("w_ff2", [4096, 1024])
    out = nc.dram_tensor("out", [NG * 512, 1024], F32, kind="ExternalOutput").ap()
    Wscr = nc.dram_tensor("wscr", [NUNITS, 128, 8, 512], BF16, kind="Internal").ap()
    Kt = nc.dram_tensor("ktscr", [8, 128, L], BF16, kind="Internal").ap()
    Vs = nc.dram_tensor("vscr", [L, 1024], BF16, kind="Internal").ap()

    with contextlib.ExitStack() as st:
        kb = KB(nc, st)
        PE, ACT, DVE, POOL, SP = kb.PE, kb.ACT, kb.DVE, kb.POOL, kb.SP

        def sb(name, shape, dt):
            return st.enter_context(nc.sbuf_tensor("sb_" + name, list(shape), dt))

        kiT = sb("kiT", [128, L], BF16)
        ident = sb("ident", [128, 128], BF16)
        ones = sb("ones", [128, 128], BF16)
        gfin = sb("gfin", [128, 1024], F32)
        vecs = sb("vecs", [128, 72], F32)
        cw = sb("cw", [128, 248], F32)
        wwi = sb("wwi", [128, 8, 8], BF16)
        KmT = sb("KmT", [128, 4, 256], BF16)
        Vm = sb("Vm", [128, 2, 512], BF16)
        pow2 = sb("pow2", [128, NIT], F32)
        small = sb("small", [128, 96], F32)
        xt = sb("xt", [128, 4, 1024], F32)
        xh = sb("xh", [32, 1024], F32)
        m1 = sb("m1", [128, 8, 512], BF16)
        qiT = sb("qiT", [128, 4, 512], BF16)
        wi = sb("wi", [128, 4, 8], F32)
        nT = sb("nT", [128, 8, 512], BF16)
        nTh = sb("nTh", [128, 8, 32], BF16)
        xn = sb("xn", [128, 2, 1024], BF16)
        B1 = sb("B1", [128, 8, 544], BF16)
        B2 = sb("B2", [128, 8, 512], BF16)
        wp = sb("wp", [128, 3, 8 * 512], BF16)
        A1 = sb("A1", [128, 8192], F32)
        A2 = sb("A2", [128, 8192], F32)
        A3 = sb("A3", [128, 6144], F32)
        psum = [st.enter_context(nc.psum_tensor("ps%d" % i, [128, 512], F32)) for i in range(8)]

        def view(ar, off, dt, shape):
            sz = {F32: 4, BF16: 2, FP8: 1}[dt]
            n = int(np.prod(shape)) * sz
            assert off % 4 == 0 and n % 4 == 0
            ap = ar[:, off // 4:(off + n) // 4]
            if dt != F32:
                ap = ap.bitcast(dt)
            if len(shape) == 2:
                ap = ap.rearrange("p (a b) -> p a b", b=shape[1])
            return ap

        def bl(name, n):
            return [Buf("%s%d" % (name, i)) for i in range(n)]

        b_kiT = bl("kiT", NCH)
        b_const = Buf("const")
        b_ident, b_ones, b_gfin, b_vecs, b_cw, b_wwi, b_KmT, b_Vm, b_pow2 = [Buf(n) for n in
            ("ident", "ones", "gfin", "vecs", "cw", "wwi", "KmT", "Vm", "pow2")]
        b_xt = bl("xt", 4)
        b_xh = Buf("xh")
        b_m1 = bl("m1", 8)
        b_qiT = bl("qiT", 4)
        b_wi = bl("wi", 4)
        b_nT = bl("nT", 4)
        b_nTh = Buf("nTh")
        b_xn = bl("xn", 2)
        b_B1 = bl("B1", 8)
        b_B2 = bl("B2", 8)
        b_wp = bl("wp", 3)
        b_ps = bl("ps", 8)
        b_wscr = bl("wscr", NUNITS)
        b_kt = Buf("ktscr")
        b_vs = Buf("vscr")
        b_out = Buf("out")
        sm_names = ["ss", "std", "rstd", "lo", "hi", "rng", "mid", "ctot", "delta", "eps"]
        sm = {n: (small[:, i:i + 1], Buf("sm_" + n)) for i, n in enumerate(sm_names)}
        cnt_ap, b_cnt = small[:, 16:24], Buf("cnt")
        st_ap, b_st = small[:, 32:32 + NIT], Buf("st")
        ssr = [(small[:, 60 + i:61 + i], Buf("ssr%d" % i)) for i in range(4)]
        rsr = [(small[:, 64 + i:65 + i], Buf("rsr%d" % i)) for i in range(4)]

        pinned = set()
        rr = [0]

        def bank():
            for _ in range(16):
                i = rr[0] % 8
                rr[0] += 1
                if i not in pinned:
                    return i
            raise RuntimeError("no psum bank")

        def pin():
            i = bank()
            pinned.add(i)
            return i

        def unpin(i):
            pinned.discard(i)

        engrr = [0]

        def ev_eng():
            engrr[0] += 1
            return ACT if engrr[0] % 2 else DVE

        def copy_op(E, out_ap, in_ap, reads, writes, scale=None):
            if E is ACT:
                kw = {} if scale is None else dict(scale=scale)
                kb.op(ACT, "activation", (out_ap, in_ap, AF.Copy), kw, reads, writes)
            else:
                if scale is None:
                    kb.op(E, "tensor_copy", (out_ap, in_ap), {}, reads, writes)
                else:
                    kb.op(E, "tensor_scalar", (out_ap, in_ap, float(scale), None, ALU.mult), {}, reads, writes)

        kb.dma(vecs[:], vecs_d, b_vecs, writes=[b_vecs])
        kb.dma(cw[:], cw_d, b_cw, writes=[b_cw])
        kb.dma(gfin[:], gfin_d.partition_broadcast(128), b_gfin, writes=[b_gfin])
        kb.dma(pow2[:], pow2_d, b_pow2, writes=[b_pow2])
        idst = view(A3, 0, F32, [128])
        b_idst = Buf("idst")
        kb.dma(idst, ident_d, b_idst, writes=[b_idst])
        kb.op(DVE, "tensor_copy", (ident[:], idst), {}, [b_idst], [b_ident])
        kb.op(DVE, "memset", (ones[:], 1.0), {}, [], [b_ones])
        kb.op(DVE, "memset", (sm["eps"][0], EPS), {}, [], [sm["eps"][1]])

        stage_in = [view(A2, i * 8192, F32, [8, 256]) for i in range(2)]
        stage_out = [view(A2, 16384 + i * 4096, BF16, [8, 256]) for i in range(2)]
        b_si = bl("si", 2)
        b_so = bl("so", 2)
        Wki = view(A2, 24576, BF16, [8, 128])
        b_Wki = Buf("Wki")
        prep_i = [0]

        def prep(src, r0, nkc, c0, ncol, gcol, dst_fn, dst_reads_writes):
            k = prep_i[0] % 2
            prep_i[0] += 1
            si, so = stage_in[k], stage_out[k]
            kb.dma(si[:, :nkc, :ncol], src[r0:r0 + nkc * 128, c0:c0 + ncol].rearrange("(kc p) c -> p kc c", p=128),
                   b_si[k], writes=[b_si[k]])
            dst, dbufs, is_dram = dst_fn()
            tgt = so[:, :nkc, :ncol] if is_dram else dst
            wr = [b_so[k]] if is_dram else dbufs
            E = DVE if prep_i[0] % 2 else POOL
            if gcol is None:
                kb.op(E, "tensor_copy", (tgt, si[:, :nkc, :ncol]), {}, [b_si[k]], wr)
            else:
                insts = []
                for kc in range(nkc):
                    insts.append(("tensor_scalar", (tgt[:, kc, :], si[:, kc, :ncol], vecs[:, gcol + kc:gcol + kc + 1], None, ALU.mult), {}))
                kb.group(E, insts, [b_si[k], b_vecs], wr)
            if is_dram:
                kb.dma(dst, so[:, :nkc, :ncol], b_so[k], reads=[b_so[k]], writes=dbufs)

        def prep_unit(src, r0, nkc, c0, gcol, uid):
            for h in range(2):
                prep(src, r0, nkc, c0 + h * 256, 256, gcol,
                     lambda h=h: (Wscr[uid][:, :nkc, h * 256:(h + 1) * 256], [b_wscr[uid]], True), None)

        def prep_res(src, c0, ncols, gcol, tile_ap, buf, dcol0=0):
            for cc in range(0, ncols, 256):
                n = min(256, ncols - cc)
                prep(src, 0, 8, c0 + cc, n, gcol,
                     lambda cc=cc, n=n: (tile_ap[:, :, dcol0 + cc:dcol0 + cc + n], [buf], False), None)

        wkv = view(A1, 0, BF16, [8, 1024])
        b_wkv = Buf("wkv")
        memT = view(A1, 16384, BF16, [8, 256])
        b_memT = Buf("memT")
        prep_res(wx_kv, 0, 1024, V_GMEM, wkv, b_wkv)

        tpi = [0]

        def norm_T(x_ap, xbuf, np_, dst_ap, dst_bufs, nrm_only=False):
            k = tpi[0] % 2
            tpi[0] += 1
            q = tpi[0] % 4
            ss_ap, ss_b = ssr[q]
            rs_ap, rs_b = rsr[q]
            xn_ap = xn[:np_, k, :]
            kb.op(ACT, "activation", (xn_ap, x_ap, AF.Square), dict(accum_out=ss_ap[:np_]), [xbuf], [b_xn[k], ss_b])
            kb.op(ACT, "activation", (rs_ap[:np_], ss_ap[:np_], AF.Sqrt), dict(scale=1.0 / 1024, bias=sm["eps"][0][:np_]),
                  [ss_b, sm["eps"][1]], [rs_b])
            kb.op(DVE, "reciprocal", (rs_ap[:np_], rs_ap[:np_]), {}, [rs_b], [rs_b])
            if nrm_only:
                return rs_ap, rs_b
            kb.op(ACT, "activation", (xn_ap, x_ap, AF.Copy), dict(scale=rs_ap[:np_]), [xbuf, rs_b], [b_xn[k]])
            bi = bank()
            pb = psum[bi][:].bitcast(BF16)
            insts = []
            for kc in range(8):
                insts.append(("transpose", (pb[:, kc * np_:(kc + 1) * np_], xn[:np_, k, kc * 128:(kc + 1) * 128], ident[:np_, :np_]), {}))
            kb.group(PE, insts, [b_xn[k], b_ident], [b_ps[bi]])
            E = ev_eng()
            copy_op(E, dst_ap, pb[:, :8 * np_].rearrange("p (k t) -> p k t", t=np_), [b_ps[bi]], dst_bufs)

        def mm_fm(bi, lhs_fn, rhs_fn, nkc, ncols, reads, N=512):
            insts = []
            for kc in range(nkc):
                insts.append(("matmul", (psum[bi][:ncols, :N], lhs_fn(kc), rhs_fn(kc)), dict(start=(kc == 0), stop=(kc == nkc - 1))))
            kb.group(PE, insts, reads, [b_ps[bi]])

        for mt in range(2):
            kb.dma(xt[:, mt, :], mem[mt * 128:(mt + 1) * 128, :], b_xt[mt], writes=[b_xt[mt]])
            norm_T(xt[:, mt, :], b_xt[mt], 128, memT[:, :, mt * 128:(mt + 1) * 128], [b_memT])
        for xhd in range(4):
            bi = bank()
            mm_fm(bi, lambda kc: wkv[:, kc, xhd * 128:(xhd + 1) * 128], lambda kc: memT[:, kc, :], 8, 128, [b_wkv, b_memT], N=256)
            copy_op(ev_eng(), KmT[:, xhd, :], psum[bi][:, :256], [b_ps[bi]], [b_KmT])
        for mt in range(2):
            bi = bank()
            mm_fm(bi, lambda kc: memT[:, kc, mt * 128:(mt + 1) * 128], lambda kc: wkv[:, kc, 512:1024], 8, 128, [b_wkv, b_memT])
            copy_op(ev_eng(), Vm[:, mt, :], psum[bi][:, :], [b_ps[bi]], [b_Vm])

        Wk = view(A1, 0, BF16, [8, 1024])
        Wv = view(A1, 16384, BF16, [8, 1024])
        b_Wk, b_Wv = Buf("Wk"), Buf("Wv")
        kb.switch([b_wkv, b_memT], [b_Wk, b_Wv])
        prep_res(w_in, 3072, 1024, V_GMIX, Wk, b_Wk)
        prep_res(w_in, 4096, 1024, V_GMIX, Wv, b_Wv)
        prep_res(w_in, 5632, 64, V_GMIX, Wki, b_Wki, 0)
        prep_res(w_in, 5632, 64, V_GMIX, Wki, b_Wki, 64)
        prep_res(w_in, 5696, 8, V_GMIX, wwi, b_wwi)

        kst = [view(A3, i * 1024, BF16, [512]) for i in range(2)]
        vst = [view(A3, 2048 + i * 2048, BF16, [1024]) for i in range(2)]
        b_kst, b_vst = bl("kst", 2), bl("vst", 2)
        kb.switch([b_idst], b_kst + b_vst)
        ksi = [0]
        for cc in range(NCH):
            for tt in range(4):
                kb.dma(xt[:, tt, :], x_seq[cc * 512 + tt * 128: cc * 512 + (tt + 1) * 128, :], b_xt[tt], writes=[b_xt[tt]])
                norm_T(xt[:, tt, :], b_xt[tt], 128, nT[:, :, tt * 128:(tt + 1) * 128], [b_nT[tt]])
            for h in range(8):
                bi = bank()
                mm_fm(bi, lambda kc: Wk[:, kc, h * 128:(h + 1) * 128], lambda kc: nT[:, kc, :], 8, 128, [b_Wk] + b_nT)
                k = ksi[0] % 2
                ksi[0] += 1
                copy_op(ev_eng(), kst[k], psum[bi][:, :], [b_ps[bi]], [b_kst[k]])
                kb.dma(Kt[h][:, cc * 512:(cc + 1) * 512], kst[k], b_kst[k], reads=[b_kst[k]], writes=[b_kt])
            for tt in range(4):
                k = (cc * 4 + tt) % 2
                for ch in range(2):
                    bi = bank()
                    mm_fm(bi, lambda kc: nT[:, kc, tt * 128:(tt + 1) * 128], lambda kc: Wv[:, kc, ch * 512:(ch + 1) * 512], 8, 128,
                          [b_Wv, b_nT[tt]])
                    copy_op(ev_eng(), vst[k][:, ch * 512:(ch + 1) * 512], psum[bi][:, :], [b_ps[bi]], [b_vst[k]])
                kb.dma(Vs[cc * 512 + tt * 128: cc * 512 + (tt + 1) * 128, :], vst[k], b_vst[k], reads=[b_vst[k]], writes=[b_vs])
            bi = bank()
            mm_fm(bi, lambda kc: Wki[:, kc, :], lambda kc: nT[:, kc, :], 8, 128, [b_Wki] + b_nT)
            copy_op(ev_eng(), kiT[:, cc * 512:(cc + 1) * 512], psum[bi][:, :], [b_ps[bi]], [b_kiT[cc]])

        for u in range(2):
            prep_unit(w_in, 0, 8, 0 + u * 512, V_GMIX, U_CA + u)
            prep_unit(w_in, 0, 8, 1024 + u * 512, V_GMIX, U_CG + u)
        for u in range(2):
            prep_unit(w_in, 0, 8, 5704 + u * 512, V_GMIX, U_GC + u)
            prep_unit(w_co, 0, 8, u * 512, None, U_CO + u)
        prep_unit(w_in, 0, 8, 5120, V_GMIX, U_QI)
        for u in range(2):
            prep_unit(w_in, 0, 8, 2048 + u * 512, V_GMIX, U_Q + u)
            prep_unit(w_in, 0, 8, 6728 + u * 512, V_GMIX, U_GA + u)
            prep_unit(w_ao, 0, 8, u * 512, None, U_AO + u)
            prep_unit(w_mix, 0, 8, u * 512, None, U_MIX + u)
        prep_unit(wx_q, 0, 8, 0, V_GX, U_XQ)
        for u in range(2):
            prep_unit(wx_o, 0, 4, u * 512, None, U_XO + u)
        for u in range(8):
            prep_unit(w_ff1, 0, 8, u * 512, V_GFFN, U_FF1 + u)
        for ch in range(2):
            for kg in range(4):
                prep_unit(w_ff2, kg * 1024, 8, ch * 512, None, U_FF2 + ch * 4 + kg)

        kb.barrier()

        wpi = [0]

        def wload(uid, nkc=8):
            k = wpi[0] % 3
            wpi[0] += 1
            w = wp[:, k, :].rearrange("p (a b) -> p a b", b=512)
            kb.dma(w[:, :nkc, :], Wscr[uid][:, :nkc, :], b_wp[k], reads=[b_wscr[uid]], writes=[b_wp[k]])
            return w, b_wp[k]

        Dg = view(A1, 0, BF16, [31, 128])
        sq = [view(A1, 7936 + i * 1024, BF16, [512]) for i in range(2)]
        gtmp = [view(A1, 9984 + i * 1024, BF16, [512]) for i in range(2)]
        meanB = view(A1, 12032, F32, [512])
        rstdB = view(A1, 14080, F32, [512])
        msq = view(A1, 16128, F32, [512])
        lnt = [view(A1, 18176 + i * 2048, F32, [512]) for i in range(2)]
        b_Dg, b_sq, b_gtmp, b_meanB, b_rstdB, b_msq, b_lnt = Buf("Dg"), bl("sq", 2), bl("gtmp", 2), Buf("meanB"), Buf("rstdB"), Buf("msq"), bl("lnt", 2)
        a1_conv = [b_Dg, b_meanB, b_rstdB, b_msq] + b_sq + b_gtmp + b_lnt
        NKB = L // 128
        maskT = view(A1, 0, FP8, [NKB, 512])
        b_mask = bl("mask", NKB // 8)
        score = A2[:, :L]
        b_score = bl("score", NCH)
        aT = view(A2, 0, BF16, [32, 512])
        b_aT = bl("aT", 32)
        rt = [view(A3, i * 2048, F32, [512]) for i in range(3)]
        mp = [view(A3, 6144 + i * 2048, BF16, [1024]) for i in range(2)]
        junk = view(A3, 10240, FP8, [1024])
        cm = view(A3, 11264, F32, [1024])
        b_rt, b_mp, b_junk, b_cm = bl("rt", 3), bl("mp", 2), Buf("junk"), Buf("cm")
        a3_idx = b_rt + b_mp + [b_junk, b_cm]
        Kbuf = [view(A3, i * 2048, BF16, [1024]) for i in range(3)]
        Vbuf = [view(A3, 6144 + i * 2048, BF16, [8, 128]) for i in range(3)]
        qTb = [view(A3, 12288 + i * 1024, BF16, [512]) for i in range(2)]
        Eb = [view(A3, 14336 + i * 1024, BF16, [512]) for i in range(3)]
        Pb = [view(A3, 17408 + i * 1024, BF16, [512]) for i in range(3)]
        rz = view(A3, 20480, F32, [512])
        b_Kbuf, b_Vbuf, b_qTb, b_Eb, b_Pb, b_rz = bl("Kbuf", 3), bl("Vbuf", 3), bl("qTb", 2), bl("Eb", 3), bl("Pb", 3), Buf("rz")
        a3_attn = b_Kbuf + b_Vbuf + b_qTb + b_Eb + b_Pb + [b_rz]
        a1_prev = [b_Wk, b_Wv]
        a2_prev = b_si + b_so + [b_Wki]
        a3_prev = b_kst + b_vst

        poolrr = [0]

        def mul_eng():
            poolrr[0] += 1
            return DVE if poolrr[0] % 2 else POOL

        def resid_add(tt, ch, bi):
            kb.op(DVE, "tensor_tensor", (xt[:, tt, ch * 512:(ch + 1) * 512], xt[:, tt, ch * 512:(ch + 1) * 512], psum[bi][:, :], ALU.add),
                  {}, [b_ps[bi], b_xt[tt]], [b_xt[tt]])

        def attn_core(n_kt, k_fn, v_fn, mask_fn, qT_ap, qT_b, dst_ap, dst_b, kv_reads_fn):
            bo, bz = pin(), pin()
            pend = []

            def flush(item, first, last):
                t, pk = item
                kb.op(PE, "matmul", (psum[bo][:, :], v_fn(t), Pb[pk]), dict(start=first, stop=last), kv_reads_fn(t) + [b_Pb[pk]], [b_ps[bo]])
                kb.op(PE, "matmul", (psum[bz][:, :], ones[:], Pb[pk]), dict(start=first, stop=last), [b_ones, b_Pb[pk]], [b_ps[bz]])

            done = 0
            for t in range(n_kt):
                bs = bank()
                kb.op(PE, "matmul", (psum[bs][:, :], k_fn(t), qT_ap), dict(start=True, stop=True), kv_reads_fn(t) + [qT_b], [b_ps[bs]])
                ek = t % 3
                mfn = mask_fn(t) if mask_fn is not None else None
                if mfn is None:
                    kb.op(ACT, "activation", (Pb[ek], psum[bs][:, :], AF.Exp), {}, [b_ps[bs]], [b_Pb[ek]])
                else:
                    kb.op(ACT, "activation", (Eb[ek], psum[bs][:, :], AF.Exp), {}, [b_ps[bs]], [b_Eb[ek]])
                    kb.op(mul_eng(), "tensor_tensor", (Pb[ek], Eb[ek], mfn[0], ALU.mult), {}, [b_Eb[ek], mfn[1]], [b_Pb[ek]])
                pend.append((t, ek))
                if len(pend) > 2:
                    flush(pend.pop(0), done == 0, False)
                    done += 1
            while pend:
                it = pend.pop(0)
                flush(it, done == 0, len(pend) == 0)
                done += 1
            kb.op(DVE, "reciprocal", (rz, psum[bz][:, :]), {}, [b_ps[bz]], [b_rz])
            kb.op(DVE, "tensor_tensor", (dst_ap, psum[bo][:, :], rz, ALU.mult), {}, [b_ps[bo], b_rz], [dst_b])
            unpin(bo)
            unpin(bz)

        for g in range(NG):
            N = (2 * g + 2) * 512
            nch = N // 512
            nkb = N // 128
            kb.dma(xh[:, :], x_own[g, 0:32, :], b_xh, writes=[b_xh])
            for tt in range(4):
                kb.dma(xt[:, tt, :], x_own[g, 32 + tt * 128: 32 + (tt + 1) * 128, :], b_xt[tt], writes=[b_xt[tt]])
            norm_T(xh[:, :], b_xh, 32, nTh[:, :, :], [b_nTh])
            for tt in range(4):
                norm_T(xt[:, tt, :], b_xt[tt], 128, nT[:, :, tt * 128:(tt + 1) * 128], [b_nT[tt]])

            kb.switch(a1_prev, a1_conv)
            a1_prev = a1_conv
            for u in range(2):
                wa, bwa = wload(U_CA + u)
                wg, bwg = wload(U_CG + u)
                for j in range(4):
                    ct = u * 4 + j
                    ba, bg, bh = bank(), bank(), bank()
                    mm_fm(ba, lambda kc: wa[:, kc, j * 128:(j + 1) * 128], lambda kc: nT[:, kc, :], 8, 128, [bwa] + b_nT)
                    mm_fm(bg, lambda kc: wg[:, kc, j * 128:(j + 1) * 128], lambda kc: nT[:, kc, :], 8, 128, [bwg] + b_nT)
                    insts = []
                    for kc in range(8):
                        insts.append(("matmul", (psum[bh][:, 0:32], wa[:, kc, j * 128:(j + 1) * 128], nTh[:, kc, :]), dict(start=(kc == 0), stop=(kc == 7))))
                    for kc in range(8):
                        insts.append(("matmul", (psum[bh][:, 32:64], wg[:, kc, j * 128:(j + 1) * 128], nTh[:, kc, :]), dict(start=(kc == 0), stop=(kc == 7))))
                    kb.group(PE, insts, [bwa, bwg, b_nTh], [b_ps[bh]])
                    k = ct % 2
                    kb.op(ACT, "activation", (gtmp[k], psum[bg][:, :], AF.Sigmoid), {}, [b_ps[bg]], [b_gtmp[k]])
                    kb.op(DVE, "tensor_tensor", (B1[:, ct, 32:544], psum[ba][:, :], gtmp[k], ALU.mult), {}, [b_ps[ba], b_gtmp[k]], [b_B1[ct]])
                    kb.op(ACT, "activation", (lnt[k][:, 0:32], psum[bh][:, 32:64], AF.Sigmoid), {}, [b_ps[bh]], [b_lnt[k]])
                    kb.op(DVE, "tensor_tensor", (B1[:, ct, 0:32], psum[bh][:, 0:32], lnt[k][:, 0:32], ALU.mult), {}, [b_ps[bh], b_lnt[k]], [b_B1[ct]])
            s1, s2 = pin(), pin()
            for ct in range(8):
                insts = []
                for k in range(31):
                    insts.append(("tensor_scalar", (Dg[:, k, :], ident[:], cw[:, ct * 31 + k: ct * 31 + k + 1], None, ALU.mult), {}))
                kb.group(POOL, insts, [b_ident, b_cw], [b_Dg])
                bc = bank()
                insts = []
                for k in range(31):
                    insts.append(("matmul", (psum[bc][:, :], Dg[:, k, :], B1[:, ct, 2 + k: 2 + k + 512]), dict(start=(k == 0), stop=(k == 30))))
                kb.group(PE, insts, [b_Dg, b_B1[ct]], [b_ps[bc]])
                k2 = ct % 2
                kb.op(ACT, "activation", (B2[:, ct, :], psum[bc][:, :], AF.Identity), dict(bias=vecs[:, V_CB + ct:V_CB + ct + 1]),
                      [b_ps[bc], b_vecs], [b_B2[ct]])
                kb.op(ACT, "activation", (sq[k2], psum[bc][:, :], AF.Square), dict(bias=vecs[:, V_CB + ct:V_CB + ct + 1]),
                      [b_ps[bc], b_vecs], [b_sq[k2]])
                kb.op(PE, "matmul", (psum[s1][:, :], ones[:], B2[:, ct, :]), dict(start=(ct == 0), stop=(ct == 7)), [b_ones, b_B2[ct]], [b_ps[s1]])
                kb.op(PE, "matmul", (psum[s2][:, :], ones[:], sq[k2]), dict(start=(ct == 0), stop=(ct == 7)), [b_ones, b_sq[k2]], [b_ps[s2]])
            kb.op(ACT, "activation", (meanB, psum[s1][:, :], AF.Copy), dict(scale=1.0 / 1024), [b_ps[s1]], [b_meanB])
            kb.op(DVE, "tensor_tensor", (msq, meanB, meanB, ALU.mult), {}, [b_meanB], [b_msq])
            kb.op(DVE, "scalar_tensor_tensor", (rstdB, psum[s2][:, :], 1.0 / 1024, msq, ALU.mult, ALU.subtract), {}, [b_ps[s2], b_msq], [b_rstdB])
            kb.op(ACT, "activation", (rstdB, rstdB, AF.Sqrt), dict(bias=sm["eps"][0]), [b_rstdB, sm["eps"][1]], [b_rstdB])
            kb.op(DVE, "reciprocal", (rstdB, rstdB), {}, [b_rstdB], [b_rstdB])
            unpin(s1)
            unpin(s2)
            for ct in range(8):
                k = ct % 2
                kb.op(DVE, "tensor_tensor", (lnt[k], B2[:, ct, :], meanB, ALU.subtract), {}, [b_B2[ct], b_meanB], [b_lnt[k]])
                kb.op(POOL, "tensor_tensor", (lnt[k], lnt[k], rstdB, ALU.mult), {}, [b_lnt[k], b_rstdB], [b_lnt[k]])
                kb.op(ACT, "activation", (B2[:, ct, :], lnt[k], AF.Silu),
                      dict(scale=vecs[:, V_LG + ct:V_LG + ct + 1], bias=vecs[:, V_LB + ct:V_LB + ct + 1]), [b_lnt[k], b_vecs], [b_B2[ct]])
            for u in range(2):
                wc, bwc = wload(U_CO + u)
                wg, bwg = wload(U_GC + u)
                for j in range(4):
                    ft = u * 4 + j
                    by, bg = bank(), bank()
                    mm_fm(by, lambda kc: wc[:, kc, j * 128:(j + 1) * 128], lambda kc: B2[:, kc, :], 8, 128, [bwc] + b_B2)
                    mm_fm(bg, lambda kc: wg[:, kc, j * 128:(j + 1) * 128], lambda kc: nT[:, kc, :], 8, 128, [bwg] + b_nT)
                    k = ft % 2
                    kb.op(ACT, "activation", (gtmp[k], psum[bg][:, :], AF.Sigmoid), dict(bias=vecs[:, V_BGC + ft:V_BGC + ft + 1]),
                          [b_ps[bg], b_vecs], [b_gtmp[k]])
                    kb.op(DVE, "tensor_tensor", (m1[:, ft, :], psum[by][:, :], gtmp[k], ALU.mult), {}, [b_ps[by], b_gtmp[k]], [b_m1[ft]])

            wq_, bwq = wload(U_QI)
            for m in range(4):
                bi = bank()
                mm_fm(bi, lambda kc: wq_[:, kc, m * 128:(m + 1) * 128], lambda kc: nT[:, kc, :], 8, 128, [bwq] + b_nT)
                copy_op(ev_eng(), qiT[:, m, :], psum[bi][:, :], [b_ps[bi]], [b_qiT[m]], scale=0.125)
            for i in range(4):
                bi = bank()
                insts = []
                for kc in range(8):
                    insts.append(("matmul", (psum[bi][:, 0:8], nT[:, kc, i * 128:(i + 1) * 128], wwi[:, kc, :]), dict(start=(kc == 0), stop=(kc == 7))))
                kb.group(PE, insts, [b_nT[i], b_wwi], [b_ps[bi]])
                copy_op(ACT, wi[:, i, :], psum[bi][:, 0:8], [b_ps[bi]], [b_wi[i]], scale=8.0 ** -0.5)
            kb.switch(a2_prev, b_score)
            a2_prev = b_score
            kb.switch(a1_prev, b_mask)
            a1_prev = b_mask
            kb.switch(a3_prev, a3_idx)
            a3_prev = a3_idx
            rti = [0]
            for i in range(4):
                kb.dma(cm, cmask[i], b_cm, writes=[b_cm])
                for c5 in range(nch):
                    for h in range(8):
                        m, s = h // 2, h % 2
                        bi = bank()
                        kb.op(PE, "matmul", (psum[bi][:, :], qiT[s * 64:(s + 1) * 64, m, i * 128:(i + 1) * 128],
                                             kiT[s * 64:(s + 1) * 64, c5 * 512:(c5 + 1) * 512]), dict(start=True, stop=True),
                              [b_qiT[m], b_kiT[c5]], [b_ps[bi]])
                        k = rti[0] % 3
                        rti[0] += 1
                        kb.op(ACT, "activation", (rt[k], psum[bi][:, :], AF.Relu), {}, [b_ps[bi]], [b_rt[k]])
                        sc = score[:, c5 * 512:(c5 + 1) * 512]
                        if h == 0:
                            kb.op(DVE, "tensor_scalar", (sc, rt[k], wi[:, i, 0:1], None, ALU.mult), {}, [b_rt[k], b_wi[i]], [b_score[c5]])
                        else:
                            kb.op(DVE, "scalar_tensor_tensor", (sc, rt[k], wi[:, i, h:h + 1], sc, ALU.mult, ALU.add), {},
                                  [b_rt[k], b_wi[i], b_score[c5]], [b_score[c5]])
                lo, b_lo = sm["lo"]
                hi, b_hi = sm["hi"]
                rng, b_rng = sm["rng"]
                mid, b_mid = sm["mid"]
                ctot, b_ctot = sm["ctot"]
                delta, b_delta = sm["delta"]
                sc_all = b_score[:nch]
                kb.op(DVE, "tensor_reduce", (lo, score[:, :N], AX.X, ALU.min), {}, sc_all, [b_lo])
                kb.op(DVE, "tensor_tensor", (score[:, N - 1024:N], score[:, N - 1024:N], cm, ALU.add), {},
                      [b_cm, b_score[nch - 2], b_score[nch - 1]], [b_score[nch - 2], b_score[nch - 1]])
                kb.op(DVE, "tensor_reduce", (hi, score[:, :N], AX.X, ALU.max), {}, sc_all, [b_hi])
                kb.op(DVE, "tensor_tensor", (rng, hi, lo, ALU.subtract), {}, [b_hi, b_lo], [b_rng])
                kb.op(DVE, "tensor_scalar", (st_ap, pow2[:, :], rng, None, ALU.mult), {}, [b_pow2, b_rng], [b_st])
                npc = N // 1024
                for it in range(NIT):
                    kb.op(DVE, "tensor_tensor", (mid, lo, st_ap[:, it:it + 1], ALU.add), {}, [b_lo, b_st], [b_mid])
                    for pc in range(npc):
                        kb.op(DVE, "tensor_scalar", (junk, score[:, pc * 1024:(pc + 1) * 1024], mid, 0.0, ALU.is_ge, ALU.add),
                              dict(accum_out=cnt_ap[:, pc:pc + 1], saturate=False), [b_score[2 * pc], b_score[2 * pc + 1], b_mid], [b_junk, b_cnt])
                    kb.op(DVE, "tensor_reduce", (ctot, cnt_ap[:, :npc], AX.X, ALU.add), {}, [b_cnt], [b_ctot])
                    kb.op(DVE, "scalar_tensor_tensor", (delta, ctot, float(KTOP) - 0.5, st_ap[:, it:it + 1], ALU.is_ge, ALU.mult), {},
                          [b_ctot, b_st], [b_delta])
                    kb.op(DVE, "tensor_tensor", (lo, lo, delta, ALU.add), {}, [b_lo, b_delta], [b_lo])
                for pc in range(npc):
                    k = pc % 2
                    kb.op(DVE, "tensor_scalar", (mp[k], score[:, pc * 1024:(pc + 1) * 1024], lo, None, ALU.is_ge), {},
                          [b_score[2 * pc], b_score[2 * pc + 1], b_lo], [b_mp[k]])
                    bi = bank()
                    pb = psum[bi][:].bitcast(BF16)
                    insts = []
                    for j in range(8):
                        insts.append(("transpose", (pb[:, j * 128:(j + 1) * 128], mp[k][:, j * 128:(j + 1) * 128], ident[:]), {}))
                    kb.group(PE, insts, [b_mp[k], b_ident], [b_ps[bi]])
                    kb.op(ACT, "activation", (maskT[:, pc * 8:(pc + 1) * 8, i * 128:(i + 1) * 128], pb.rearrange("p (k q) -> p k q", q=128), AF.Copy),
                          dict(saturate=False), [b_ps[bi]], [b_mask[pc]])

            kb.switch(a3_prev, a3_attn)
            a3_prev = a3_attn
            kvi = [0]
            for u in range(2):
                wq, bwq = wload(U_Q + u)
                for j in range(4):
                    h = u * 4 + j
                    bi = bank()
                    mm_fm(bi, lambda kc: wq[:, kc, j * 128:(j + 1) * 128], lambda kc: nT[:, kc, :], 8, 128, [bwq] + b_nT)
                    qk = h % 2
                    copy_op(ev_eng(), qTb[qk], psum[bi][:, :], [b_ps[bi]], [b_qTb[qk]], scale=128.0 ** -0.5)
                    kvmap = {}
                    state = {"next": 0}

                    def ensure(ku, h=h, kvmap=kvmap, state=state):
                        while state["next"] <= ku:
                            kk = kvi[0] % 3
                            kvi[0] += 1
                            n_ = state["next"]
                            kb.dma(Kbuf[kk], Kt[h][:, n_ * 1024:(n_ + 1) * 1024], b_Kbuf[kk], reads=[b_kt], writes=[b_Kbuf[kk]])
                            kb.dma(Vbuf[kk], Vs[n_ * 1024:(n_ + 1) * 1024, h * 128:(h + 1) * 128].rearrange("(kb p) d -> p kb d", p=128),
                                   b_Vbuf[kk], reads=[b_vs], writes=[b_Vbuf[kk]])
                            kvmap[n_] = kk
                            state["next"] += 1

                    def k_fn(t, kvmap=kvmap, ensure=ensure):
                        ensure(t // 8)
                        return Kbuf[kvmap[t // 8]][:, (t % 8) * 128:(t % 8 + 1) * 128]

                    def v_fn(t, kvmap=kvmap):
                        return Vbuf[kvmap[t // 8]][:, t % 8, :]

                    def kv_reads(t, kvmap=kvmap):
                        return [b_Kbuf[kvmap[t // 8]], b_Vbuf[kvmap[t // 8]]]

                    def mask_fn(t):
                        return (maskT[:, t, :], b_mask[t // 8])

                    attn_core(nkb, k_fn, v_fn, mask_fn, qTb[qk], b_qTb[qk], B1[:, h, 0:512], b_B1[h], kv_reads)

            for u in range(2):
                wc, bwc = wload(U_AO + u)
                wg, bwg = wload(U_GA + u)
                for j in range(4):
                    ft = u * 4 + j
                    by, bg = bank(), bank()
                    mm_fm(by, lambda kc: wc[:, kc, j * 128:(j + 1) * 128], lambda kc: B1[:, kc, 0:512], 8, 128, [bwc] + b_B1)
                    mm_fm(bg, lambda kc: wg[:, kc, j * 128:(j + 1) * 128], lambda kc: nT[:, kc, :], 8, 128, [bwg] + b_nT)
                    k = ft % 2
                    kb.op(ACT, "activation", (qTb[k], psum[bg][:, :], AF.Sigmoid), dict(bias=vecs[:, V_BGA + ft:V_BGA + ft + 1]),
                          [b_ps[bg], b_vecs], [b_qTb[k]])
                    kb.op(DVE, "tensor_tensor", (B2[:, ft, :], psum[by][:, :], qTb[k], ALU.mult), {}, [b_ps[by], b_qTb[k]], [b_B2[ft]])

            for ch in range(2):
                wm, bwm = wload(U_MIX + ch)
                for tt in range(4):
                    bi = bank()
                    insts = []
                    for kc in range(8):
                        insts.append(("matmul", (psum[bi][:, :], m1[:, kc, tt * 128:(tt + 1) * 128], wm[:, kc, :]), dict(start=(kc == 0), stop=False)))
                    for kc in range(8):
                        insts.append(("matmul", (psum[bi][:, :], B2[:, kc, tt * 128:(tt + 1) * 128], wm[:, kc, :]), dict(start=False, stop=(kc == 7))))
                    kb.group(PE, insts, [bwm] + b_m1 + b_B2, [b_ps[bi]])
                    resid_add(tt, ch, bi)

            for tt in range(4):
                norm_T(xt[:, tt, :], b_xt[tt], 128, nT[:, :, tt * 128:(tt + 1) * 128], [b_nT[tt]])
            wq, bwq = wload(U_XQ)
            for xhd in range(4):
                bi = bank()
                mm_fm(bi, lambda kc: wq[:, kc, xhd * 128:(xhd + 1) * 128], lambda kc: nT[:, kc, :], 8, 128, [bwq] + b_nT)
                qk = xhd % 2
                copy_op(ev_eng(), qTb[qk], psum[bi][:, :], [b_ps[bi]], [b_qTb[qk]], scale=128.0 ** -0.5)
                attn_core(2, lambda t: KmT[:, xhd, t * 128:(t + 1) * 128], lambda t: Vm[:, t, xhd * 128:(xhd + 1) * 128], None,
                          qTb[qk], b_qTb[qk], B1[:, xhd, 0:512], b_B1[xhd], lambda t: [b_KmT, b_Vm])
            for ch in range(2):
                wo, bwo = wload(U_XO + ch, nkc=4)
                for tt in range(4):
                    bi = bank()
                    insts = []
                    for kc in range(4):
                        insts.append(("matmul", (psum[bi][:, :], B1[:, kc, tt * 128:(tt + 1) * 128], wo[:, kc, :]), dict(start=(kc == 0), stop=(kc == 3))))
                    kb.group(PE, insts, [bwo] + b_B1[:4], [b_ps[bi]])
                    resid_add(tt, ch, bi)

            for tt in range(4):
                norm_T(xt[:, tt, :], b_xt[tt], 128, nT[:, :, tt * 128:(tt + 1) * 128], [b_nT[tt]])
            kb.switch(a2_prev, b_aT)
            a2_prev = b_aT
            for u in range(8):
                w1, bw1 = wload(U_FF1 + u)
                for j in range(4):
                    fft = u * 4 + j
                    bi = bank()
                    mm_fm(bi, lambda kc: w1[:, kc, j * 128:(j + 1) * 128], lambda kc: nT[:, kc, :], 8, 128, [bw1] + b_nT)
                    k = fft % 3
                    kb.op(ACT, "activation", (Eb[k], psum[bi][:, :], AF.Relu), {}, [b_ps[bi]], [b_Eb[k]])
                    kb.op(mul_eng(), "tensor_tensor", (aT[:, fft, :], Eb[k], Eb[k], ALU.mult), {}, [b_Eb[k]], [b_aT[fft]])
            for ch in range(2):
                acc = [pin() for _ in range(4)]
                for kg in range(4):
                    w2, bw2 = wload(U_FF2 + ch * 4 + kg)
                    for tt in range(4):
                        insts = []
                        for kc in range(8):
                            insts.append(("matmul", (psum[acc[tt]][:, :], aT[:, kg * 8 + kc, tt * 128:(tt + 1) * 128], w2[:, kc, :]),
                                          dict(start=(kg == 0 and kc == 0), stop=(kg == 3 and kc == 7))))
                        kb.group(PE, insts, [bw2] + b_aT[kg * 8:(kg + 1) * 8], [b_ps[acc[tt]]])
                for tt in range(4):
                    resid_add(tt, ch, acc[tt])
                    unpin(acc[tt])

            for tt in range(4):
                rs_ap, rs_b = norm_T(xt[:, tt, :], b_xt[tt], 128, None, None, nrm_only=True)
                kb.op(DVE, "scalar_tensor_tensor", (xt[:, tt, :], xt[:, tt, :], rs_ap, gfin[:], ALU.mult, ALU.mult), {},
                      [b_xt[tt], rs_b, b_gfin], [b_xt[tt]])
                kb.dma(out[g * 512 + tt * 128: g * 512 + (tt + 1) * 128, :], xt[:, tt, :], b_xt[tt], reads=[b_xt[tt]], writes=[b_out])

        kb.barrier()

        with nc.Block() as block:
            @block.sync
            def _(e):
                kb.replay(SP, e)

            @block.tensor
            def _(e):
                kb.replay(PE, e)

            @block.scalar
            def _(e):
                kb.replay(ACT, e)

            @block.vector
            def _(e):
                kb.replay(DVE, e)

            @block.gpsimd
            def _(e):
                kb.replay(POOL, e)

        print("ops: " + ", ".join("%s=%d" % (E.name, len(E.ops)) for E in kb.engs), "sems", kb.nsem, flush=True)
    return nc


def host_inputs(inputs, NG):
    L = 1024 * NG
    f = lambda a: np.ascontiguousarray(np.asarray(a, dtype=np.float32))
    x = f(inputs["x"])[:, :L]
    B = x.shape[0]
    mem = f(inputs["mem"])

    def fm(v):
        v = f(v).reshape(-1, 128)
        return v.T

    vecs = np.concatenate([fm(inputs["norm_mix_g"][0]), fm(inputs["norm_x_g"][0]), fm(inputs["norm_mem_g"][0]),
                           fm(inputs["norm_ffn_g"][0]), fm(inputs["b_gate"][0][:1024]), fm(inputs["b_gate"][0][1024:]),
                           fm(inputs["conv_b"][0]), fm(inputs["conv_ln_g"][0]), fm(inputs["conv_ln_b"][0])], axis=1)
    cwfm = f(inputs["conv_w"][0]).T.reshape(8, 128, 31).transpose(1, 0, 2).reshape(128, 248)
    shared = dict(
        vecs=f(vecs), cwfm=f(cwfm), gfin=f(inputs["norm_final_g"]).reshape(1024),
        ident=np.eye(128, dtype=np.float32),
        pow2=f(np.tile((0.5 ** np.arange(1, NIT + 1))[None, :], (128, 1))),
        w_in=f(inputs["w_in"][0]), w_conv_out=f(inputs["w_conv_out"][0]), w_attn_out=f(inputs["w_attn_out"][0]),
        w_mix_out=f(inputs["w_mix_out"][0]), wx_q=f(inputs["wx_q"][0]), wx_kv=f(inputs["wx_kv"][0]), wx_o=f(inputs["wx_o"][0]),
        w_ff1=f(inputs["w_ff1"][0]), w_ff2=f(inputs["w_ff2"][0]),
    )
    r = np.arange(128)[:, None]
    cms = []
    for p in range(2):
        cm = np.zeros((4, 128, 1024), np.float32)
        kcol = np.arange(1024)[None, :]
        for i in range(4):
            qpos = p * 512 + i * 128 + r
            cm[i] = np.where(kcol <= qpos, 0.0, NEG)
        cms.append(cm)
    in_maps = []
    for core in range(2 * B):
        b, p = core // 2, core % 2
        xo = np.zeros((NG, 544, 1024), np.float32)
        for g in range(NG):
            c = 2 * g + p
            s = c * 512 - 32
            if s < 0:
                xo[g, 32:] = x[b, 0:512]
            else:
                xo[g] = x[b, s:s + 544]
        d = dict(shared)
        d.update(x_seq=f(x[b]), x_own=xo, mem=f(mem[b]), cmask=cms[p])
        in_maps.append(d)
    return in_maps


_NC_CACHE = {}


def kernel_impl(inputs, NG):
    if NG not in _NC_CACHE:
        _NC_CACHE[NG] = build(NG)
    nc = _NC_CACHE[NG]
    in_maps = host_inputs(inputs, NG)
    n = len(in_maps)
    res = run_bass_kernel_spmd(nc, in_maps, core_ids=list(range(n)))
    B = n // 2
    L = 1024 * NG
    o = np.zeros((B, L, 1024), np.float32)
    for core in range(n):
        b, p = core // 2, core % 2
        r = np.asarray(res.results[core]["out"], dtype=np.float32)
        for g in range(NG):
            c = 2 * g + p
            o[b, c * 512:(c + 1) * 512] = r[g * 512:(g + 1) * 512]
    return o


def kernel(**inputs):
    return kernel_impl(inputs, 8)
```

```python
import contextlib
import numpy as np
import concourse.bass as bass
import concourse.mybir as mybir
from concourse.bass_utils import run_bass_kernel_spmd

F32 = mybir.dt.float32
BF16 = mybir.dt.bfloat16
FP8 = mybir.dt.float8e4
AF = mybir.ActivationFunctionType
ALU = mybir.AluOpType
AX = mybir.AxisListType

NIT = 18
KTOP = 256
EPS = 1e-6
NEG = -1.0e30

U_CA, U_CG, U_Q, U_QI, U_GC, U_GA, U_CO, U_AO, U_MIX, U_XQ, U_XO, U_FF1, U_FF2 = 0, 2, 4, 6, 7, 9, 11, 13, 15, 17, 18, 20, 28
NUNITS = 36
V_GMIX, V_GX, V_GMEM, V_GFFN, V_BGC, V_BGA, V_CB, V_LG, V_LB = 0, 8, 16, 24, 32, 40, 48, 56, 64


class Buf:
    __slots__ = ("name", "w", "r", "dsem", "dcnt")

    def __init__(self, name):
        self.name = name
        self.w = {}
        self.r = {}
        self.dsem = None
        self.dcnt = 0


class Eng:
    def __init__(self, name, sem, selfwait):
        self.name = name
        self.sem = sem
        self.count = 0
        self.seen = {}
        self.ops = []
        self.selfwait = selfwait


def _merge(d, s):
    for k, (sem, val) in s.items():
        if k not in d or d[k][1] < val:
            d[k] = (sem, val)


class KB:
    def __init__(self, nc, stack):
        self.nc = nc
        self.stack = stack
        self.nsem = 0
        self.PE = Eng("pe", self.new_sem(), False)
        self.ACT = Eng("act", self.new_sem(), True)
        self.DVE = Eng("dve", self.new_sem(), True)
        self.POOL = Eng("pool", self.new_sem(), True)
        self.SP = Eng("sp", self.new_sem(), False)
        self.engs = [self.PE, self.ACT, self.DVE, self.POOL, self.SP]
        self.dbufs = []
        self.stage = 'init'
        self.pe_labels = []

    def new_sem(self):
        self.nsem += 1
        return self.stack.enter_context(self.nc.semaphore("s%d" % self.nsem))

    def _wait(self, E, deps):
        for key, (sem, val) in deps.items():
            if sem is E.sem and not E.selfwait:
                continue
            if E.seen.get(key, 0) >= val:
                continue
            E.seen[key] = val
            E.ops.append(("wait", sem, val))

    def _deps(self, reads, writes):
        deps = {}
        for b in reads:
            _merge(deps, b.w)
        for b in writes:
            _merge(deps, b.w)
            _merge(deps, b.r)
        return deps

    def _commit(self, tk, reads, writes):
        key = id(tk[0])
        for b in reads:
            if key not in b.r or b.r[key][1] < tk[1]:
                b.r[key] = tk
        for b in writes:
            b.w = {key: tk}
            b.r = {}

    def group(self, E, insts, reads=(), writes=()):
        self._wait(E, self._deps(reads, writes))
        E.count += 1
        tk = (E.sem, E.count)
        n = len(insts)
        for i, (m, a, k) in enumerate(insts):
            E.ops.append(("op", m, a, k, E.sem if i == n - 1 else None, 1))
            if E is self.PE:
                self.pe_labels.append(self.stage)
        self._commit(tk, reads, writes)

    def op(self, E, method, args, kwargs=None, reads=(), writes=()):
        self.group(E, [(method, args, kwargs or {})], reads, writes)

    def dma(self, out, in_, sb, reads=(), writes=(), Q=None):
        Q = Q or self.SP
        deps = self._deps(reads, writes)
        if sb.dsem is None:
            sb.dsem = self.new_sem()
            self.dbufs.append(sb)
        if sb.dcnt > 0:
            _merge(deps, {id(sb.dsem): (sb.dsem, sb.dcnt * 16)})
        self._wait(Q, deps)
        sb.dcnt += 1
        tk = (sb.dsem, sb.dcnt * 16)
        Q.ops.append(("op", "dma_start", (), dict(out=out, in_=in_), sb.dsem, 16))
        self._commit(tk, reads, writes)

    def switch(self, old, new):
        deps = {}
        for b in old:
            _merge(deps, b.w)
            _merge(deps, b.r)
        for b in new:
            d = dict(deps)
            _merge(d, b.w)
            _merge(d, b.r)
            b.w = d
            b.r = {}

    def barrier(self):
        deps = {}
        for E in self.engs:
            if E.count > 0:
                deps[id(E.sem)] = (E.sem, E.count)
        for b in self.dbufs:
            deps[id(b.dsem)] = (b.dsem, b.dcnt * 16)
        for E in self.engs:
            self._wait(E, deps)

    def replay(self, E, e):
        for o in E.ops:
            if o[0] == "wait":
                e.wait_ge(o[1], o[2])
            else:
                _, m, a, k, sem, inc = o
                ins = getattr(e, m)(*a, **k)
                if sem is not None:
                    ins.then_inc(sem, inc)


def build(NG):
    L = 1024 * NG
    NCH = 2 * NG
    nc = bass.Bass("TRN2", target_bir_lowering=False)

    def din(name, shape, dt=F32):
        return nc.dram_tensor(name, list(shape), dt, kind="ExternalInput").ap()

    x_seq = din("x_seq", [L, 1024])
    x_own = din("x_own", [NG, 544, 1024])
    mem = din("mem", [256, 1024])
    cmask = din("cmask", [4, 128, 1024])
    vecs_d = din("vecs", [128, 72])
    cw_d = din("cwfm", [128, 248])
    gfin_d = din("gfin", [1024])
    ident_d = din("ident", [128, 128])
    pow2_d = din("pow2", [128, NIT])
    w_in = din("w_in", [1024, 7752])
    w_co = din("w_conv_out", [1024, 1024])
    w_ao = din("w_attn_out", [1024, 1024])
    w_mix = din("w_mix_out", [1024, 1024])
    wx_q = din("wx_q", [1024, 512])
    wx_kv = din("wx_kv", [1024, 1024])
    wx_o = din("wx_o", [512, 1024])
    w_ff1 = din("w_ff1", [1024, 4096])
    w_ff2 = din("w_ff2", [4096, 1024])
    out = nc.dram_tensor("out", [NG * 512, 1024], F32, kind="ExternalOutput").ap()
    Wscr = nc.dram_tensor("wscr", [NUNITS, 128, 8, 512], BF16, kind="Internal").ap()
    Kt = nc.dram_tensor("ktscr", [8, 128, L], BF16, kind="Internal").ap()
    Vs = nc.dram_tensor("vscr", [L, 1024], BF16, kind="Internal").ap()

    with contextlib.ExitStack() as st:
        kb = KB(nc, st)
        PE, ACT, DVE, POOL, SP = kb.PE, kb.ACT, kb.DVE, kb.POOL, kb.SP

        def sb(name, shape, dt):
            return st.enter_context(nc.sbuf_tensor("sb_" + name, list(shape), dt))

        kiT = sb("kiT", [128, L], BF16)
        ident = sb("ident", [128, 128], BF16)
        ones = sb("ones", [128, 128], BF16)
        gfin = sb("gfin", [128, 1024], F32)
        vecs = sb("vecs", [128, 72], F32)
        cw = sb("cw", [128, 248], F32)
        wwi = sb("wwi", [128, 8, 8], BF16)
        KmT = sb("KmT", [128, 4, 256], BF16)
        Vm = sb("Vm", [128, 2, 512], BF16)
        pow2 = sb("pow2", [128, NIT], F32)
        small = sb("small", [128, 96], F32)
        xt = sb("xt", [128, 4, 1024], F32)
        xh = sb("xh", [32, 1024], F32)
        m1 = sb("m1", [128, 8, 512], BF16)
        qiT = sb("qiT", [128, 4, 512], BF16)
        wi = sb("wi", [128, 4, 8], F32)
        nT = sb("nT", [128, 8, 512], BF16)
        nTh = sb("nTh", [128, 8, 32], BF16)
        xn = sb("xn", [128, 2, 1024], BF16)
        B1 = sb("B1", [128, 8, 544], BF16)
        B2 = sb("B2", [128, 8, 512], BF16)
        wp = sb("wp", [128, 3, 8 * 512], BF16)
        A1 = sb("A1", [128, 8192], F32)
        A2 = sb("A2", [128, 8192], F32)
        A3 = sb("A3", [128, 6144], F32)
        psum = [st.enter_context(nc.psum_tensor("ps%d" % i, [128, 512], F32)) for i in range(8)]

        def view(ar, off, dt, shape):
            sz = {F32: 4, BF16: 2, FP8: 1}[dt]
            n = int(np.prod(shape)) * sz
            assert off % 4 == 0 and n % 4 == 0
            ap = ar[:, off // 4:(off + n) // 4]
            if dt != F32:
                ap = ap.bitcast(dt)
            if len(shape) == 2:
                ap = ap.rearrange("p (a b) -> p a b", b=shape[1])
            return ap

        def bl(name, n):
            return [Buf("%s%d" % (name, i)) for i in range(n)]

        b_kiT = bl("kiT", NCH)
        b_const = Buf("const")
        b_ident, b_ones, b_gfin, b_vecs, b_cw, b_wwi, b_KmT, b_Vm, b_pow2 = [Buf(n) for n in
            ("ident", "ones", "gfin", "vecs", "cw", "wwi", "KmT", "Vm", "pow2")]
        b_xt = bl("xt", 4)
        b_xh = Buf("xh")
        b_m1 = bl("m1", 8)
        b_qiT = bl("qiT", 4)
        b_wi = bl("wi", 4)
        b_nT = bl("nT", 4)
        b_nTh = Buf("nTh")
        b_xn = bl("xn", 2)
        b_B1 = bl("B1", 8)
        b_B2 = bl("B2", 8)
        b_wp = bl("wp", 3)
        b_ps = bl("ps", 8)
        b_wscr = bl("wscr", NUNITS)
        b_kt = [bl("kt%d_" % c, 8) for c in range(NCH)]
        b_vs = [bl("vs%d_" % c, 4) for c in range(NCH)]
        b_out = Buf("out")
        sm_names = ["ss", "std", "rstd", "lo", "hi", "rng", "mid", "ctot", "delta", "eps", "nmid", "sA"]
        sm = {n: (small[:, i:i + 1], Buf("sm_" + n)) for i, n in enumerate(sm_names)}
        cnt_ap, b_cnt = small[:, 16:24], Buf("cnt")
        cntA_ap, b_cntA = small[:, 24:32], Buf("cntA")
        st_ap, b_st = small[:, 32:32 + NIT], Buf("st")
        ssr = [(small[:, 60 + i:61 + i], Buf("ssr%d" % i)) for i in range(4)]
        rsr = [(small[:, 64 + i:65 + i], Buf("rsr%d" % i)) for i in range(4)]

        pinned = set()
        rr = [0]

        def bank():
            for _ in range(16):
                i = rr[0] % 8
                rr[0] += 1
                if i not in pinned:
                    return i
            raise RuntimeError("no psum bank")

        def pin():
            i = bank()
            pinned.add(i)
            return i

        def unpin(i):
            pinned.discard(i)

        engrr = [0]

        def ev_eng():
            engrr[0] += 1
            return ACT if engrr[0] % 2 else DVE

        def copy_op(E, out_ap, in_ap, reads, writes, scale=None):
            if E is ACT:
                kw = {} if scale is None else dict(scale=scale)
                kb.op(ACT, "activation", (out_ap, in_ap, AF.Copy), kw, reads, writes)
            else:
                if scale is None:
                    kb.op(E, "tensor_copy", (out_ap, in_ap), {}, reads, writes)
                else:
                    kb.op(E, "tensor_scalar", (out_ap, in_ap, float(scale), None, ALU.mult), {}, reads, writes)

        kb.dma(vecs[:], vecs_d, b_vecs, writes=[b_vecs])
        kb.dma(cw[:], cw_d, b_cw, writes=[b_cw])
        kb.dma(gfin[:], gfin_d.partition_broadcast(128), b_gfin, writes=[b_gfin])
        kb.dma(pow2[:], pow2_d, b_pow2, writes=[b_pow2])
        idst = view(A3, 0, F32, [128])
        b_idst = Buf("idst")
        kb.dma(idst, ident_d, b_idst, writes=[b_idst])
        kb.op(DVE, "tensor_copy", (ident[:], idst), {}, [b_idst], [b_ident])
        kb.op(DVE, "memset", (ones[:], 1.0), {}, [], [b_ones])
        kb.op(DVE, "memset", (sm["eps"][0], EPS), {}, [], [sm["eps"][1]])

        stage_in = [view(A2, i * 8192, F32, [8, 256]) for i in range(2)]
        stage_out = [view(A2, 16384 + i * 4096, BF16, [8, 256]) for i in range(2)]
        b_si = bl("si", 2)
        b_so = bl("so", 2)
        Wki = view(A2, 24576, BF16, [8, 128])
        b_Wki = Buf("Wki")
        prep_i = [0]

        def prep(src, r0, nkc, c0, ncol, gcol, dst_fn, dst_reads_writes):
            k = prep_i[0] % 2
            prep_i[0] += 1
            si, so = stage_in[k], stage_out[k]
            kb.dma(si[:, :nkc, :ncol], src[r0:r0 + nkc * 128, c0:c0 + ncol].rearrange("(kc p) c -> p kc c", p=128),
                   b_si[k], writes=[b_si[k]])
            dst, dbufs, is_dram = dst_fn()
            tgt = so[:, :nkc, :ncol] if is_dram else dst
            wr = [b_so[k]] if is_dram else dbufs
            if gcol is None:
                E = POOL if prep_i[0] % 3 else DVE
                kb.op(E, "tensor_copy", (tgt, si[:, :nkc, :ncol]), {}, [b_si[k]], wr)
            else:
                E = DVE if prep_i[0] % 2 else ACT
                insts = []
                for kc in range(nkc):
                    if E is ACT:
                        insts.append(("activation", (tgt[:, kc, :], si[:, kc, :ncol], AF.Copy), dict(scale=vecs[:, gcol + kc:gcol + kc + 1])))
                    else:
                        insts.append(("tensor_scalar", (tgt[:, kc, :], si[:, kc, :ncol], vecs[:, gcol + kc:gcol + kc + 1], None, ALU.mult), {}))
                kb.group(E, insts, [b_si[k], b_vecs], wr)
            if is_dram:
                kb.dma(dst, so[:, :nkc, :ncol], b_so[k], reads=[b_so[k]], writes=dbufs)

        def prep_unit(src, r0, nkc, c0, gcol, uid):
            for h in range(2):
                prep(src, r0, nkc, c0 + h * 256, 256, gcol,
                     lambda h=h: (Wscr[uid][:, :nkc, h * 256:(h + 1) * 256], [b_wscr[uid]], True), None)

        def prep_res(src, c0, ncols, gcol, tile_ap, buf, dcol0=0):
            for cc in range(0, ncols, 256):
                n = min(256, ncols - cc)
                prep(src, 0, 8, c0 + cc, n, gcol,
                     lambda cc=cc, n=n: (tile_ap[:, :, dcol0 + cc:dcol0 + cc + n], [buf], False), None)

        kb.stage = 'setup'
        wkv = view(A1, 0, BF16, [8, 1024])
        b_wkv = Buf("wkv")
        memT = view(A1, 16384, BF16, [8, 256])
        b_memT = Buf("memT")
        prep_res(wx_kv, 0, 1024, V_GMEM, wkv, b_wkv)

        tpi = [0]

        def norm_T(x_ap, xbuf, np_, dst_ap, dst_bufs, nrm_only=False):
            k = tpi[0] % 2
            tpi[0] += 1
            q = tpi[0] % 4
            ss_ap, ss_b = ssr[q]
            rs_ap, rs_b = rsr[q]
            xn_ap = xn[:np_, k, :]
            kb.op(ACT, "activation", (xn_ap, x_ap, AF.Square), dict(accum_out=ss_ap[:np_]), [xbuf], [b_xn[k], ss_b])
            kb.op(ACT, "activation", (rs_ap[:np_], ss_ap[:np_], AF.Sqrt), dict(scale=1.0 / 1024, bias=sm["eps"][0][:np_]),
                  [ss_b, sm["eps"][1]], [rs_b])
            kb.op(DVE, "reciprocal", (rs_ap[:np_], rs_ap[:np_]), {}, [rs_b], [rs_b])
            if nrm_only:
                return rs_ap, rs_b
            kb.op(ACT, "activation", (xn_ap, x_ap, AF.Copy), dict(scale=rs_ap[:np_]), [xbuf, rs_b], [b_xn[k]])
            bi = bank()
            pb = psum[bi][:].bitcast(BF16)
            insts = []
            for kc in range(8):
                insts.append(("transpose", (pb[:, kc * np_:(kc + 1) * np_], xn[:np_, k, kc * 128:(kc + 1) * 128], ident[:np_, :np_]), {}))
            kb.group(PE, insts, [b_xn[k], b_ident], [b_ps[bi]])
            E = ev_eng()
            copy_op(E, dst_ap, pb[:, :8 * np_].rearrange("p (k t) -> p k t", t=np_), [b_ps[bi]], dst_bufs)

        def mm_fm(bi, lhs_fn, rhs_fn, nkc, ncols, reads, N=512):
            insts = []
            for kc in range(nkc):
                insts.append(("matmul", (psum[bi][:ncols, :N], lhs_fn(kc), rhs_fn(kc)), dict(start=(kc == 0), stop=(kc == nkc - 1))))
            kb.group(PE, insts, reads, [b_ps[bi]])

        for mt in range(2):
            kb.dma(xt[:, mt, :], mem[mt * 128:(mt + 1) * 128, :], b_xt[mt], writes=[b_xt[mt]])
            norm_T(xt[:, mt, :], b_xt[mt], 128, memT[:, :, mt * 128:(mt + 1) * 128], [b_memT])
        for xhd in range(4):
            bi = bank()
            mm_fm(bi, lambda kc: wkv[:, kc, xhd * 128:(xhd + 1) * 128], lambda kc: memT[:, kc, :], 8, 128, [b_wkv, b_memT], N=256)
            copy_op(ev_eng(), KmT[:, xhd, :], psum[bi][:, :256], [b_ps[bi]], [b_KmT])
        for mt in range(2):
            bi = bank()
            mm_fm(bi, lambda kc: memT[:, kc, mt * 128:(mt + 1) * 128], lambda kc: wkv[:, kc, 512:1024], 8, 128, [b_wkv, b_memT])
            copy_op(ev_eng(), Vm[:, mt, :], psum[bi][:, :], [b_ps[bi]], [b_Vm])

        Wk = view(A1, 0, BF16, [8, 1024])
        Wv = view(A1, 16384, BF16, [8, 1024])
        b_Wk, b_Wv = Buf("Wk"), Buf("Wv")
        kb.switch([b_wkv, b_memT], [b_Wk, b_Wv])
        prep_res(w_in, 3072, 1024, V_GMIX, Wk, b_Wk)
        prep_res(w_in, 4096, 1024, V_GMIX, Wv, b_Wv)
        prep_res(w_in, 5632, 64, V_GMIX, Wki, b_Wki, 0)
        prep_res(w_in, 5632, 64, V_GMIX, Wki, b_Wki, 64)
        prep_res(w_in, 5696, 8, V_GMIX, wwi, b_wwi)

        kb.stage = 'phaseA'
        kst = [view(A3, i * 1024, BF16, [512]) for i in range(2)]
        vst = [view(A3, 2048 + i * 2048, BF16, [1024]) for i in range(2)]
        b_kst, b_vst = bl("kst", 2), bl("vst", 2)
        kb.switch([b_idst], b_kst + b_vst)
        ksi = [0]
        for tt in range(4):
            kb.dma(xt[:, tt, :], x_seq[tt * 128:(tt + 1) * 128, :], b_xt[tt], writes=[b_xt[tt]])
        for cc in range(NCH):
            for tt in range(4):
                norm_T(xt[:, tt, :], b_xt[tt], 128, nT[:, :, tt * 128:(tt + 1) * 128], [b_nT[tt]])
                if cc + 1 < NCH:
                    kb.dma(xt[:, tt, :], x_seq[(cc + 1) * 512 + tt * 128: (cc + 1) * 512 + (tt + 1) * 128, :], b_xt[tt], writes=[b_xt[tt]])
            for h in range(8):
                bi = bank()
                mm_fm(bi, lambda kc: Wk[:, kc, h * 128:(h + 1) * 128], lambda kc: nT[:, kc, :], 8, 128, [b_Wk] + b_nT)
                k = ksi[0] % 2
                ksi[0] += 1
                copy_op(ev_eng(), kst[k], psum[bi][:, :], [b_ps[bi]], [b_kst[k]])
                kb.dma(Kt[h][:, cc * 512:(cc + 1) * 512], kst[k], b_kst[k], reads=[b_kst[k]], writes=[b_kt[cc][h]])
            for tt in range(4):
                k = (cc * 4 + tt) % 2
                for ch in range(2):
                    bi = bank()
                    mm_fm(bi, lambda kc: nT[:, kc, tt * 128:(tt + 1) * 128], lambda kc: Wv[:, kc, ch * 512:(ch + 1) * 512], 8, 128,
                          [b_Wv, b_nT[tt]])
                    copy_op(ev_eng(), vst[k][:, ch * 512:(ch + 1) * 512], psum[bi][:, :], [b_ps[bi]], [b_vst[k]])
                kb.dma(Vs[cc * 512 + tt * 128: cc * 512 + (tt + 1) * 128, :], vst[k], b_vst[k], reads=[b_vst[k]], writes=[b_vs[cc][tt]])
            bi = bank()
            mm_fm(bi, lambda kc: Wki[:, kc, :], lambda kc: nT[:, kc, :], 8, 128, [b_Wki] + b_nT)
            copy_op(ev_eng(), kiT[:, cc * 512:(cc + 1) * 512], psum[bi][:, :], [b_ps[bi]], [b_kiT[cc]])

        kb.stage = 'prep'
        for u in range(2):
            prep_unit(w_in, 0, 8, 0 + u * 512, V_GMIX, U_CA + u)
            prep_unit(w_in, 0, 8, 1024 + u * 512, V_GMIX, U_CG + u)
        for u in range(2):
            prep_unit(w_in, 0, 8, 5704 + u * 512, V_GMIX, U_GC + u)
            prep_unit(w_co, 0, 8, u * 512, None, U_CO + u)
        prep_unit(w_in, 0, 8, 5120, V_GMIX, U_QI)
        for u in range(2):
            prep_unit(w_in, 0, 8, 2048 + u * 512, V_GMIX, U_Q + u)
            prep_unit(w_in, 0, 8, 6728 + u * 512, V_GMIX, U_GA + u)
            prep_unit(w_ao, 0, 8, u * 512, None, U_AO + u)
            prep_unit(w_mix, 0, 8, u * 512, None, U_MIX + u)
        prep_unit(wx_q, 0, 8, 0, V_GX, U_XQ)
        for u in range(2):
            prep_unit(wx_o, 0, 4, u * 512, None, U_XO + u)
        for u in range(8):
            prep_unit(w_ff1, 0, 8, u * 512, V_GFFN, U_FF1 + u)
        for ch in range(2):
            for kg in range(4):
                prep_unit(w_ff2, kg * 1024, 8, ch * 512, None, U_FF2 + ch * 4 + kg)

        wpi = [0]

        def wload(uid, nkc=8):
            k = wpi[0] % 3
            wpi[0] += 1
            w = wp[:, k, :].rearrange("p (a b) -> p a b", b=512)
            kb.dma(w[:, :nkc, :], Wscr[uid][:, :nkc, :], b_wp[k], reads=[b_wscr[uid]], writes=[b_wp[k]])
            return w, b_wp[k]

        Dg = view(A1, 0, BF16, [31, 128])
        sq = [view(A1, 7936 + i * 1024, BF16, [512]) for i in range(2)]
        gtmp = [view(A1, 9984 + i * 1024, BF16, [512]) for i in range(2)]
        meanB = view(A1, 12032, F32, [512])
        rstdB = view(A1, 14080, F32, [512])
        msq = view(A1, 16128, F32, [512])
        lnt = [view(A1, 18176 + i * 2048, F32, [512]) for i in range(2)]
        b_Dg, b_sq, b_gtmp, b_meanB, b_rstdB, b_msq, b_lnt = Buf("Dg"), bl("sq", 2), bl("gtmp", 2), Buf("meanB"), Buf("rstdB"), Buf("msq"), bl("lnt", 2)
        a1_conv = [b_Dg, b_meanB, b_rstdB, b_msq] + b_sq + b_gtmp + b_lnt
        NKB = L // 128
        maskT = view(A1, 0, FP8, [NKB, 512])
        b_mask = bl("mask", NKB // 8)
        score = A2[:, :L]
        b_score = bl("score", NCH)
        aT = view(A2, 0, BF16, [32, 512])
        b_aT = bl("aT", 32)
        rt = [view(A3, i * 2048, F32, [512]) for i in range(3)]
        mp = [view(A3, 6144 + i * 2048, BF16, [1024]) for i in range(2)]
        junk = view(A3, 10240, FP8, [1024])
        cm = view(A3, 11264, F32, [1024])
        junk2 = view(A3, 15360, FP8, [1024])
        b_junk2 = Buf("junk2")
        b_rt, b_mp, b_junk, b_cm = bl("rt", 3), bl("mp", 2), Buf("junk"), Buf("cm")
        a3_idx = b_rt + b_mp + [b_junk, b_cm, b_junk2]
        Kbuf = [view(A3, i * 2048, BF16, [1024]) for i in range(3)]
        Vbuf = [view(A3, 6144 + i * 2048, BF16, [8, 128]) for i in range(3)]
        qTb = [view(A3, 12288 + i * 1024, BF16, [512]) for i in range(2)]
        Eb = [view(A3, 14336 + i * 1024, BF16, [512]) for i in range(3)]
        Pb = [view(A3, 17408 + i * 1024, BF16, [512]) for i in range(3)]
        rz = view(A3, 20480, F32, [512])
        b_Kbuf, b_Vbuf, b_qTb, b_Eb, b_Pb, b_rz = bl("Kbuf", 3), bl("Vbuf", 3), bl("qTb", 2), bl("Eb", 3), bl("Pb", 3), Buf("rz")
        a3_attn = b_Kbuf + b_Vbuf + b_qTb + b_Eb + b_Pb + [b_rz]
        a1_prev = [b_Wk, b_Wv]
        a2_prev = b_si + b_so + [b_Wki]
        a3_prev = b_kst + b_vst

        poolrr = [0]

        def mul_eng():
            poolrr[0] += 1
            return POOL if poolrr[0] % 3 == 0 else DVE

        def resid_add(tt, ch, bi):
            kb.op(DVE, "tensor_tensor", (xt[:, tt, ch * 512:(ch + 1) * 512], xt[:, tt, ch * 512:(ch + 1) * 512], psum[bi][:, :], ALU.add),
                  {}, [b_ps[bi], b_xt[tt]], [b_xt[tt]])

        def attn_core(n_kt, k_fn, v_fn, mask_fn, qT_ap, qT_b, dst_ap, dst_b, kv_reads_fn):
            bo, bz = pin(), pin()
            pend = []

            def flush(item, first, last):
                t, pk = item
                kb.op(PE, "matmul", (psum[bo][:, :], v_fn(t), Pb[pk]), dict(start=first, stop=last), kv_reads_fn(t) + [b_Pb[pk]], [b_ps[bo]])
                kb.op(PE, "matmul", (psum[bz][:, :], ones[:], Pb[pk]), dict(start=first, stop=last), [b_ones, b_Pb[pk]], [b_ps[bz]])

            done = 0
            for t in range(n_kt):
                bs = bank()
                kb.op(PE, "matmul", (psum[bs][:, :], k_fn(t), qT_ap), dict(start=True, stop=True), kv_reads_fn(t) + [qT_b], [b_ps[bs]])
                ek = t % 3
                mfn = mask_fn(t) if mask_fn is not None else None
                if mfn is None:
                    kb.op(ACT, "activation", (Pb[ek], psum[bs][:, :], AF.Exp), {}, [b_ps[bs]], [b_Pb[ek]])
                else:
                    kb.op(ACT, "activation", (Eb[ek], psum[bs][:, :], AF.Exp), {}, [b_ps[bs]], [b_Eb[ek]])
                    kb.op(mul_eng(), "tensor_tensor", (Pb[ek], Eb[ek], mfn[0], ALU.mult), {}, [b_Eb[ek], mfn[1]], [b_Pb[ek]])
                pend.append((t, ek))
                if len(pend) > 2:
                    flush(pend.pop(0), done == 0, False)
                    done += 1
            while pend:
                it = pend.pop(0)
                flush(it, done == 0, len(pend) == 0)
                done += 1
            kb.op(DVE, "reciprocal", (rz, psum[bz][:, :]), {}, [b_ps[bz]], [b_rz])
            kb.op(DVE, "tensor_tensor", (dst_ap, psum[bo][:, :], rz, ALU.mult), {}, [b_ps[bo], b_rz], [dst_b])
            unpin(bo)
            unpin(bz)

        for g in range(NG):
            N = (2 * g + 2) * 512
            nch = N // 512
            nkb = N // 128
            kb.stage = 'g%d.G1' % g
            kb.dma(xh[:, :], x_own[g, 0:32, :], b_xh, writes=[b_xh])
            for tt in range(4):
                kb.dma(xt[:, tt, :], x_own[g, 32 + tt * 128: 32 + (tt + 1) * 128, :], b_xt[tt], writes=[b_xt[tt]])
            norm_T(xh[:, :], b_xh, 32, nTh[:, :, :], [b_nTh])
            for tt in range(4):
                norm_T(xt[:, tt, :], b_xt[tt], 128, nT[:, :, tt * 128:(tt + 1) * 128], [b_nT[tt]])

            kb.stage = 'g%d.G2' % g
            kb.switch(a1_prev, a1_conv)
            a1_prev = a1_conv
            for u in range(2):
                wa, bwa = wload(U_CA + u)
                wg, bwg = wload(U_CG + u)
                for j in range(4):
                    ct = u * 4 + j
                    ba, bg, bh = bank(), bank(), bank()
                    mm_fm(ba, lambda kc: wa[:, kc, j * 128:(j + 1) * 128], lambda kc: nT[:, kc, :], 8, 128, [bwa] + b_nT)
                    mm_fm(bg, lambda kc: wg[:, kc, j * 128:(j + 1) * 128], lambda kc: nT[:, kc, :], 8, 128, [bwg] + b_nT)
                    insts = []
                    for kc in range(8):
                        insts.append(("matmul", (psum[bh][:, 0:32], wa[:, kc, j * 128:(j + 1) * 128], nTh[:, kc, :]), dict(start=(kc == 0), stop=(kc == 7))))
                    for kc in range(8):
                        insts.append(("matmul", (psum[bh][:, 32:64], wg[:, kc, j * 128:(j + 1) * 128], nTh[:, kc, :]), dict(start=(kc == 0), stop=(kc == 7))))
                    kb.group(PE, insts, [bwa, bwg, b_nTh], [b_ps[bh]])
                    k = ct % 2
                    kb.op(ACT, "activation", (gtmp[k], psum[bg][:, :], AF.Sigmoid), {}, [b_ps[bg]], [b_gtmp[k]])
                    kb.op(DVE, "tensor_tensor", (B1[:, ct, 32:544], psum[ba][:, :], gtmp[k], ALU.mult), {}, [b_ps[ba], b_gtmp[k]], [b_B1[ct]])
                    kb.op(ACT, "activation", (lnt[k][:, 0:32], psum[bh][:, 32:64], AF.Sigmoid), {}, [b_ps[bh]], [b_lnt[k]])
                    kb.op(DVE, "tensor_tensor", (B1[:, ct, 0:32], psum[bh][:, 0:32], lnt[k][:, 0:32], ALU.mult), {}, [b_ps[bh], b_lnt[k]], [b_B1[ct]])
            s1, s2 = pin(), pin()
            for ct in range(8):
                kb.op(DVE, "tensor_tensor", (Dg, ident[:].unsqueeze(1).to_broadcast([128, 31, 128]),
                                             cw[:, ct * 31:(ct + 1) * 31].unsqueeze(2).to_broadcast([128, 31, 128]), ALU.mult), {},
                      [b_ident, b_cw], [b_Dg])
                bc = bank()
                insts = []
                for k in range(31):
                    insts.append(("matmul", (psum[bc][:, :], Dg[:, k, :], B1[:, ct, 2 + k: 2 + k + 512]), dict(start=(k == 0), stop=(k == 30))))
                kb.group(PE, insts, [b_Dg, b_B1[ct]], [b_ps[bc]])
                k2 = ct % 2
                kb.op(ACT, "activation", (B2[:, ct, :], psum[bc][:, :], AF.Identity), dict(bias=vecs[:, V_CB + ct:V_CB + ct + 1]),
                      [b_ps[bc], b_vecs], [b_B2[ct]])
                kb.op(ACT, "activation", (sq[k2], psum[bc][:, :], AF.Square), dict(bias=vecs[:, V_CB + ct:V_CB + ct + 1]),
                      [b_ps[bc], b_vecs], [b_sq[k2]])
                kb.op(PE, "matmul", (psum[s1][:, :], ones[:], B2[:, ct, :]), dict(start=(ct == 0), stop=(ct == 7)), [b_ones, b_B2[ct]], [b_ps[s1]])
                kb.op(PE, "matmul", (psum[s2][:, :], ones[:], sq[k2]), dict(start=(ct == 0), stop=(ct == 7)), [b_ones, b_sq[k2]], [b_ps[s2]])
            kb.op(ACT, "activation", (meanB, psum[s1][:, :], AF.Copy), dict(scale=1.0 / 1024), [b_ps[s1]], [b_meanB])
            kb.op(DVE, "tensor_tensor", (msq, meanB, meanB, ALU.mult), {}, [b_meanB], [b_msq])
            kb.op(DVE, "scalar_tensor_tensor", (rstdB, psum[s2][:, :], 1.0 / 1024, msq, ALU.mult, ALU.subtract), {}, [b_ps[s2], b_msq], [b_rstdB])
            kb.op(ACT, "activation", (rstdB, rstdB, AF.Sqrt), dict(bias=sm["eps"][0]), [b_rstdB, sm["eps"][1]], [b_rstdB])
            kb.op(DVE, "reciprocal", (rstdB, rstdB), {}, [b_rstdB], [b_rstdB])
            unpin(s1)
            unpin(s2)
            for ct in range(8):
                k = ct % 2
                kb.op(DVE, "tensor_tensor", (lnt[k], B2[:, ct, :], meanB, ALU.subtract), {}, [b_B2[ct], b_meanB], [b_lnt[k]])
                kb.op(POOL, "tensor_tensor", (lnt[k], lnt[k], rstdB, ALU.mult), {}, [b_lnt[k], b_rstdB], [b_lnt[k]])
                kb.op(ACT, "activation", (B2[:, ct, :], lnt[k], AF.Silu),
                      dict(scale=vecs[:, V_LG + ct:V_LG + ct + 1], bias=vecs[:, V_LB + ct:V_LB + ct + 1]), [b_lnt[k], b_vecs], [b_B2[ct]])
            for u in range(2):
                wc, bwc = wload(U_CO + u)
                wg, bwg = wload(U_GC + u)
                for j in range(4):
                    ft = u * 4 + j
                    by, bg = bank(), bank()
                    mm_fm(by, lambda kc: wc[:, kc, j * 128:(j + 1) * 128], lambda kc: B2[:, kc, :], 8, 128, [bwc] + b_B2)
                    mm_fm(bg, lambda kc: wg[:, kc, j * 128:(j + 1) * 128], lambda kc: nT[:, kc, :], 8, 128, [bwg] + b_nT)
                    k = ft % 2
                    kb.op(ACT, "activation", (gtmp[k], psum[bg][:, :], AF.Sigmoid), dict(bias=vecs[:, V_BGC + ft:V_BGC + ft + 1]),
                          [b_ps[bg], b_vecs], [b_gtmp[k]])
                    kb.op(DVE, "tensor_tensor", (m1[:, ft, :], psum[by][:, :], gtmp[k], ALU.mult), {}, [b_ps[by], b_gtmp[k]], [b_m1[ft]])

            kb.stage = 'g%d.G3' % g
            wq_, bwq = wload(U_QI)
            for m in range(4):
                bi = bank()
                mm_fm(bi, lambda kc: wq_[:, kc, m * 128:(m + 1) * 128], lambda kc: nT[:, kc, :], 8, 128, [bwq] + b_nT)
                copy_op(ev_eng(), qiT[:, m, :], psum[bi][:, :], [b_ps[bi]], [b_qiT[m]], scale=0.125)
            for i in range(4):
                bi = bank()
                insts = []
                for kc in range(8):
                    insts.append(("matmul", (psum[bi][:, 0:8], nT[:, kc, i * 128:(i + 1) * 128], wwi[:, kc, :]), dict(start=(kc == 0), stop=(kc == 7))))
                kb.group(PE, insts, [b_nT[i], b_wwi], [b_ps[bi]])
                copy_op(ACT, wi[:, i, :], psum[bi][:, 0:8], [b_ps[bi]], [b_wi[i]], scale=8.0 ** -0.5)
            kb.switch(a2_prev, b_score)
            a2_prev = b_score
            kb.switch(a1_prev, b_mask)
            a1_prev = b_mask
            kb.switch(a3_prev, a3_idx)
            a3_prev = a3_idx
            rti = [0]
            for i in range(4):
                kb.dma(cm, cmask[i], b_cm, writes=[b_cm])
                for c5 in range(nch):
                    for h in range(8):
                        m, s = h // 2, h % 2
                        bi = bank()
                        kb.op(PE, "matmul", (psum[bi][:, :], qiT[s * 64:(s + 1) * 64, m, i * 128:(i + 1) * 128],
                                             kiT[s * 64:(s + 1) * 64, c5 * 512:(c5 + 1) * 512]), dict(start=True, stop=True),
                              [b_qiT[m], b_kiT[c5]], [b_ps[bi]])
                        k = rti[0] % 3
                        rti[0] += 1
                        kb.op(ACT, "activation", (rt[k], psum[bi][:, :], AF.Relu), {}, [b_ps[bi]], [b_rt[k]])
                        sc = score[:, c5 * 512:(c5 + 1) * 512]
                        if h == 0:
                            kb.op(DVE, "tensor_scalar", (sc, rt[k], wi[:, i, 0:1], None, ALU.mult), {}, [b_rt[k], b_wi[i]], [b_score[c5]])
                        else:
                            kb.op(DVE, "scalar_tensor_tensor", (sc, rt[k], wi[:, i, h:h + 1], sc, ALU.mult, ALU.add), {},
                                  [b_rt[k], b_wi[i], b_score[c5]], [b_score[c5]])
                lo, b_lo = sm["lo"]
                hi, b_hi = sm["hi"]
                rng, b_rng = sm["rng"]
                mid, b_mid = sm["mid"]
                nmid, b_nmid = sm["nmid"]
                sA, b_sA = sm["sA"]
                ctot, b_ctot = sm["ctot"]
                delta, b_delta = sm["delta"]
                sc_all = b_score[:nch]
                kb.op(DVE, "tensor_reduce", (lo, score[:, :N], AX.X, ALU.min), {}, sc_all, [b_lo])
                kb.op(DVE, "tensor_tensor", (score[:, N - 1024:N], score[:, N - 1024:N], cm, ALU.add), {},
                      [b_cm, b_score[nch - 2], b_score[nch - 1]], [b_score[nch - 2], b_score[nch - 1]])
                kb.op(DVE, "tensor_reduce", (hi, score[:, :N], AX.X, ALU.max), {}, sc_all, [b_hi])
                kb.op(DVE, "tensor_tensor", (rng, hi, lo, ALU.subtract), {}, [b_hi, b_lo], [b_rng])
                kb.op(DVE, "tensor_scalar", (st_ap, pow2[:, :], rng, None, ALU.mult), {}, [b_pow2, b_rng], [b_st])
                npc = N // 1024
                a_pcs = [pc for pc in range(npc) if pc % 2 == 0]
                d_pcs = [pc for pc in range(npc) if pc % 2 == 1]
                thr2 = 2.0 * (float(KTOP) - 0.5) - 1024.0 * len(a_pcs)
                for it in range(NIT):
                    kb.op(DVE, "tensor_tensor", (mid, lo, st_ap[:, it:it + 1], ALU.add), {}, [b_lo, b_st], [b_mid])
                    kb.op(DVE, "tensor_scalar", (nmid, mid, -1.0, None, ALU.mult), {}, [b_mid], [b_nmid])
                    for j, pc in enumerate(a_pcs):
                        kb.op(ACT, "activation", (junk2, score[:, pc * 1024:(pc + 1) * 1024], AF.Sign),
                              dict(bias=nmid, accum_out=cntA_ap[:, j:j + 1], saturate=False),
                              [b_score[2 * pc], b_score[2 * pc + 1], b_nmid], [b_junk2, b_cntA])
                    for j, pc in enumerate(d_pcs):
                        kb.op(DVE, "tensor_scalar", (junk, score[:, pc * 1024:(pc + 1) * 1024], mid, 0.0, ALU.is_ge, ALU.add),
                              dict(accum_out=cnt_ap[:, j:j + 1], saturate=False), [b_score[2 * pc], b_score[2 * pc + 1], b_mid], [b_junk, b_cnt])
                    kb.op(DVE, "tensor_reduce", (sA, cntA_ap[:, :len(a_pcs)], AX.X, ALU.add), {}, [b_cntA], [b_sA])
                    if d_pcs:
                        kb.op(DVE, "tensor_reduce", (ctot, cnt_ap[:, :len(d_pcs)], AX.X, ALU.add), {}, [b_cnt], [b_ctot])
                        kb.op(DVE, "scalar_tensor_tensor", (sA, ctot, 2.0, sA, ALU.mult, ALU.add), {}, [b_ctot, b_sA], [b_sA])
                    kb.op(DVE, "scalar_tensor_tensor", (delta, sA, thr2, st_ap[:, it:it + 1], ALU.is_ge, ALU.mult), {},
                          [b_sA, b_st], [b_delta])
                    kb.op(DVE, "tensor_tensor", (lo, lo, delta, ALU.add), {}, [b_lo, b_delta], [b_lo])
                for pc in range(npc):
                    k = pc % 2
                    kb.op(DVE, "tensor_scalar", (mp[k], score[:, pc * 1024:(pc + 1) * 1024], lo, None, ALU.is_ge), {},
                          [b_score[2 * pc], b_score[2 * pc + 1], b_lo], [b_mp[k]])
                    bi = bank()
                    pb = psum[bi][:].bitcast(BF16)
                    insts = []
                    for j in range(8):
                        insts.append(("transpose", (pb[:, j * 128:(j + 1) * 128], mp[k][:, j * 128:(j + 1) * 128], ident[:]), {}))
                    kb.group(PE, insts, [b_mp[k], b_ident], [b_ps[bi]])
                    kb.op(ACT, "activation", (maskT[:, pc * 8:(pc + 1) * 8, i * 128:(i + 1) * 128], pb.rearrange("p (k q) -> p k q", q=128), AF.Copy),
                          dict(saturate=False), [b_ps[bi]], [b_mask[pc]])

            kb.stage = 'g%d.G4' % g
            kb.switch(a3_prev, a3_attn)
            a3_prev = a3_attn
            kvi = [0]
            for u in range(2):
                wq, bwq = wload(U_Q + u)
                for j in range(4):
                    h = u * 4 + j
                    bi = bank()
                    mm_fm(bi, lambda kc: wq[:, kc, j * 128:(j + 1) * 128], lambda kc: nT[:, kc, :], 8, 128, [bwq] + b_nT)
                    qk = h % 2
                    copy_op(ev_eng(), qTb[qk], psum[bi][:, :], [b_ps[bi]], [b_qTb[qk]], scale=128.0 ** -0.5)
                    kvmap = {}
                    state = {"next": 0}

                    def ensure(ku, h=h, kvmap=kvmap, state=state):
                        while state["next"] <= ku:
                            kk = kvi[0] % 3
                            kvi[0] += 1
                            n_ = state["next"]
                            kb.dma(Kbuf[kk], Kt[h][:, n_ * 1024:(n_ + 1) * 1024], b_Kbuf[kk],
                                   reads=[b_kt[2 * n_][h], b_kt[2 * n_ + 1][h]], writes=[b_Kbuf[kk]])
                            kb.dma(Vbuf[kk], Vs[n_ * 1024:(n_ + 1) * 1024, h * 128:(h + 1) * 128].rearrange("(kb p) d -> p kb d", p=128),
                                   b_Vbuf[kk], reads=b_vs[2 * n_] + b_vs[2 * n_ + 1], writes=[b_Vbuf[kk]])
                            kvmap[n_] = kk
                            state["next"] += 1

                    def k_fn(t, kvmap=kvmap, ensure=ensure):
                        ensure(t // 8)
                        return Kbuf[kvmap[t // 8]][:, (t % 8) * 128:(t % 8 + 1) * 128]

                    def v_fn(t, kvmap=kvmap):
                        return Vbuf[kvmap[t // 8]][:, t % 8, :]

                    def kv_reads(t, kvmap=kvmap):
                        return [b_Kbuf[kvmap[t // 8]], b_Vbuf[kvmap[t // 8]]]

                    def mask_fn(t):
                        return (maskT[:, t, :], b_mask[t // 8])

                    attn_core(nkb, k_fn, v_fn, mask_fn, qTb[qk], b_qTb[qk], B1[:, h, 0:512], b_B1[h], kv_reads)

            kb.stage = 'g%d.G5' % g
            for u in range(2):
                wc, bwc = wload(U_AO + u)
                wg, bwg = wload(U_GA + u)
                for j in range(4):
                    ft = u * 4 + j
                    by, bg = bank(), bank()
                    mm_fm(by, lambda kc: wc[:, kc, j * 128:(j + 1) * 128], lambda kc: B1[:, kc, 0:512], 8, 128, [bwc] + b_B1)
                    mm_fm(bg, lambda kc: wg[:, kc, j * 128:(j + 1) * 128], lambda kc: nT[:, kc, :], 8, 128, [bwg] + b_nT)
                    k = ft % 2
                    kb.op(ACT, "activation", (qTb[k], psum[bg][:, :], AF.Sigmoid), dict(bias=vecs[:, V_BGA + ft:V_BGA + ft + 1]),
                          [b_ps[bg], b_vecs], [b_qTb[k]])
                    kb.op(DVE, "tensor_tensor", (B2[:, ft, :], psum[by][:, :], qTb[k], ALU.mult), {}, [b_ps[by], b_qTb[k]], [b_B2[ft]])

            kb.stage = 'g%d.G6' % g
            for ch in range(2):
                wm, bwm = wload(U_MIX + ch)
                for tt in range(4):
                    bi = bank()
                    insts = []
                    for kc in range(8):
                        insts.append(("matmul", (psum[bi][:, :], m1[:, kc, tt * 128:(tt + 1) * 128], wm[:, kc, :]), dict(start=(kc == 0), stop=False)))
                    for kc in range(8):
                        insts.append(("matmul", (psum[bi][:, :], B2[:, kc, tt * 128:(tt + 1) * 128], wm[:, kc, :]), dict(start=False, stop=(kc == 7))))
                    kb.group(PE, insts, [bwm] + b_m1 + b_B2, [b_ps[bi]])
                    resid_add(tt, ch, bi)

            kb.stage = 'g%d.G7' % g
            for tt in range(4):
                norm_T(xt[:, tt, :], b_xt[tt], 128, nT[:, :, tt * 128:(tt + 1) * 128], [b_nT[tt]])
            wq, bwq = wload(U_XQ)
            for xhd in range(4):
                bi = bank()
                mm_fm(bi, lambda kc: wq[:, kc, xhd * 128:(xhd + 1) * 128], lambda kc: nT[:, kc, :], 8, 128, [bwq] + b_nT)
                qk = xhd % 2
                copy_op(ev_eng(), qTb[qk], psum[bi][:, :], [b_ps[bi]], [b_qTb[qk]], scale=128.0 ** -0.5)
                attn_core(2, lambda t: KmT[:, xhd, t * 128:(t + 1) * 128], lambda t: Vm[:, t, xhd * 128:(xhd + 1) * 128], None,
                          qTb[qk], b_qTb[qk], B1[:, xhd, 0:512], b_B1[xhd], lambda t: [b_KmT, b_Vm])
            for ch in range(2):
                wo, bwo = wload(U_XO + ch, nkc=4)
                for tt in range(4):
                    bi = bank()
                    insts = []
                    for kc in range(4):
                        insts.append(("matmul", (psum[bi][:, :], B1[:, kc, tt * 128:(tt + 1) * 128], wo[:, kc, :]), dict(start=(kc == 0), stop=(kc == 3))))
                    kb.group(PE, insts, [bwo] + b_B1[:4], [b_ps[bi]])
                    resid_add(tt, ch, bi)

            kb.stage = 'g%d.G8' % g
            for tt in range(4):
                norm_T(xt[:, tt, :], b_xt[tt], 128, nT[:, :, tt * 128:(tt + 1) * 128], [b_nT[tt]])
            kb.switch(a2_prev, b_aT)
            a2_prev = b_aT
            for u in range(8):
                w1, bw1 = wload(U_FF1 + u)
                for j in range(4):
                    fft = u * 4 + j
                    bi = bank()
                    mm_fm(bi, lambda kc: w1[:, kc, j * 128:(j + 1) * 128], lambda kc: nT[:, kc, :], 8, 128, [bw1] + b_nT)
                    k = fft % 3
                    kb.op(ACT, "activation", (Eb[k], psum[bi][:, :], AF.Relu), {}, [b_ps[bi]], [b_Eb[k]])
                    kb.op(mul_eng(), "tensor_tensor", (aT[:, fft, :], Eb[k], Eb[k], ALU.mult), {}, [b_Eb[k]], [b_aT[fft]])
            for ch in range(2):
                acc = [pin() for _ in range(4)]
                for kg in range(4):
                    w2, bw2 = wload(U_FF2 + ch * 4 + kg)
                    for tt in range(4):
                        insts = []
                        for kc in range(8):
                            insts.append(("matmul", (psum[acc[tt]][:, :], aT[:, kg * 8 + kc, tt * 128:(tt + 1) * 128], w2[:, kc, :]),
                                          dict(start=(kg == 0 and kc == 0), stop=(kg == 3 and kc == 7))))
                        kb.group(PE, insts, [bw2] + b_aT[kg * 8:(kg + 1) * 8], [b_ps[acc[tt]]])
                for tt in range(4):
                    resid_add(tt, ch, acc[tt])
                    unpin(acc[tt])

            kb.stage = 'g%d.G9' % g
            for tt in range(4):
                rs_ap, rs_b = norm_T(xt[:, tt, :], b_xt[tt], 128, None, None, nrm_only=True)
                kb.op(DVE, "scalar_tensor_tensor", (xt[:, tt, :], xt[:, tt, :], rs_ap, gfin[:], ALU.mult, ALU.mult), {},
                      [b_xt[tt], rs_b, b_gfin], [b_xt[tt]])
                kb.dma(out[g * 512 + tt * 128: g * 512 + (tt + 1) * 128, :], xt[:, tt, :], b_xt[tt], reads=[b_xt[tt]], writes=[b_out])

        kb.barrier()

        with nc.Block() as block:
            @block.sync
            def _(e):
                kb.replay(SP, e)

            @block.tensor
            def _(e):
                kb.replay(PE, e)

            @block.scalar
            def _(e):
                kb.replay(ACT, e)

            @block.vector
            def _(e):
                kb.replay(DVE, e)

            @block.gpsimd
            def _(e):
                kb.replay(POOL, e)

        nc._pe_labels = kb.pe_labels
        print("ops: " + ", ".join("%s=%d" % (E.name, len(E.ops)) for E in kb.engs), "sems", kb.nsem, flush=True)
    return nc


def host_inputs(inputs, NG):
    L = 1024 * NG
    f = lambda a: np.ascontiguousarray(np.asarray(a, dtype=np.float32))
    x = f(inputs["x"])[:, :L]
    B = x.shape[0]
    mem = f(inputs["mem"])

    def fm(v):
        v = f(v).reshape(-1, 128)
        return v.T

    vecs = np.concatenate([fm(inputs["norm_mix_g"][0]), fm(inputs["norm_x_g"][0]), fm(inputs["norm_mem_g"][0]),
                           fm(inputs["norm_ffn_g"][0]), fm(inputs["b_gate"][0][:1024]), fm(inputs["b_gate"][0][1024:]),
                           fm(inputs["conv_b"][0]), fm(inputs["conv_ln_g"][0]), fm(inputs["conv_ln_b"][0])], axis=1)
    cwfm = f(inputs["conv_w"][0]).T.reshape(8, 128, 31).transpose(1, 0, 2).reshape(128, 248)
    shared = dict(
        vecs=f(vecs), cwfm=f(cwfm), gfin=f(inputs["norm_final_g"]).reshape(1024),
        ident=np.eye(128, dtype=np.float32),
        pow2=f(np.tile((0.5 ** np.arange(1, NIT + 1))[None, :], (128, 1))),
        w_in=f(inputs["w_in"][0]), w_conv_out=f(inputs["w_conv_out"][0]), w_attn_out=f(inputs["w_attn_out"][0]),
        w_mix_out=f(inputs["w_mix_out"][0]), wx_q=f(inputs["wx_q"][0]), wx_kv=f(inputs["wx_kv"][0]), wx_o=f(inputs["wx_o"][0]),
        w_ff1=f(inputs["w_ff1"][0]), w_ff2=f(inputs["w_ff2"][0]),
    )
    r = np.arange(128)[:, None]
    cms = []
    for p in range(2):
        cm = np.zeros((4, 128, 1024), np.float32)
        kcol = np.arange(1024)[None, :]
        for i in range(4):
            qpos = p * 512 + i * 128 + r
            cm[i] = np.where(kcol <= qpos, 0.0, NEG)
        cms.append(cm)
    in_maps = []
    for core in range(2 * B):
        b, p = core // 2, core % 2
        xo = np.zeros((NG, 544, 1024), np.float32)
        for g in range(NG):
            c = 2 * g + p
            s = c * 512 - 32
            if s < 0:
                xo[g, 32:] = x[b, 0:512]
            else:
                xo[g] = x[b, s:s + 544]
        d = dict(shared)
        d.update(x_seq=f(x[b]), x_own=xo, mem=f(mem[b]), cmask=cms[p])
        in_maps.append(d)
    return in_maps


_NC_CACHE = {}


def kernel_impl(inputs, NG):
    if NG not in _NC_CACHE:
        _NC_CACHE[NG] = build(NG)
    nc = _NC_CACHE[NG]
    in_maps = host_inputs(inputs, NG)
    n = len(in_maps)
    res = run_bass_kernel_spmd(nc, in_maps, core_ids=list(range(n)))
    B = n // 2
    L = 1024 * NG
    o = np.zeros((B, L, 1024), np.float32)
    for core in range(n):
        b, p = core // 2, core % 2
        r = np.asarray(res.results[core]["out"], dtype=np.float32)
        for g in range(NG):
            c = 2 * g + p
            o[b, c * 512:(c + 1) * 512] = r[g * 512:(g + 1) * 512]
    return o


def kernel(**inputs):
    return kernel_impl(inputs, 8)
```

```python
import contextlib
import numpy as np
import concourse.bass as bass
import concourse.mybir as mybir
from concourse.bass_utils import run_bass_kernel_spmd

F32 = mybir.dt.float32
BF16 = mybir.dt.bfloat16
FP8 = mybir.dt.float8e4
AF = mybir.ActivationFunctionType
ALU = mybir.AluOpType
AX = mybir.AxisListType

NIT = 16
KTOP = 256
EPS = 1e-6
NEG = -1.0e30

U_CA, U_CG, U_Q, U_QI, U_GC, U_GA, U_CO, U_AO, U_MIX, U_XQ, U_XO, U_FF1, U_FF2 = 0, 2, 4, 6, 7, 9, 11, 13, 15, 17, 18, 20, 28
NUNITS = 36
V_GMIX, V_GX, V_GMEM, V_GFFN, V_BGC, V_BGA, V_CB, V_LG, V_LB = 0, 8, 16, 24, 32, 40, 48, 56, 64


class Buf:
    __slots__ = ("name", "w", "r", "dsem", "dcnt")

    def __init__(self, name):
        self.name = name
        self.w = {}
        self.r = {}
        self.dsem = None
        self.dcnt = 0


class Eng:
    def __init__(self, name, sem, selfwait):
        self.name = name
        self.sem = sem
        self.count = 0
        self.seen = {}
        self.ops = []
        self.selfwait = selfwait


def _merge(d, s):
    for k, (sem, val) in s.items():
        if k not in d or d[k][1] < val:
            d[k] = (sem, val)


class KB:
    def __init__(self, nc, stack):
        self.nc = nc
        self.stack = stack
        self.nsem = 0
        self.PE = Eng("pe", self.new_sem(), False)
        self.ACT = Eng("act", self.new_sem(), True)
        self.DVE = Eng("dve", self.new_sem(), True)
        self.POOL = Eng("pool", self.new_sem(), True)
        self.SP = Eng("sp", self.new_sem(), False)
        self.engs = [self.PE, self.ACT, self.DVE, self.POOL, self.SP]
        self.dbufs = []
        self.stage = 'init'
        self.pe_labels = []

    def new_sem(self):
        self.nsem += 1
        return self.stack.enter_context(self.nc.semaphore("s%d" % self.nsem))

    def _wait(self, E, deps):
        for key, (sem, val) in deps.items():
            if sem is E.sem and not E.selfwait:
                continue
            if E.seen.get(key, 0) >= val:
                continue
            E.seen[key] = val
            E.ops.append(("wait", sem, val))

    def _deps(self, reads, writes):
        deps = {}
        for b in reads:
            _merge(deps, b.w)
        for b in writes:
            _merge(deps, b.w)
            _merge(deps, b.r)
        return deps

    def _commit(self, tk, reads, writes):
        key = id(tk[0])
        for b in reads:
            if key not in b.r or b.r[key][1] < tk[1]:
                b.r[key] = tk
        for b in writes:
            b.w = {key: tk}
            b.r = {}

    def group(self, E, insts, reads=(), writes=()):
        self._wait(E, self._deps(reads, writes))
        E.count += 1
        tk = (E.sem, E.count)
        n = len(insts)
        for i, (m, a, k) in enumerate(insts):
            E.ops.append(("op", m, a, k, E.sem if i == n - 1 else None, 1))
            if E is self.PE:
                self.pe_labels.append(self.stage)
        self._commit(tk, reads, writes)

    def op(self, E, method, args, kwargs=None, reads=(), writes=()):
        self.group(E, [(method, args, kwargs or {})], reads, writes)

    def dma(self, out, in_, sb, reads=(), writes=(), Q=None):
        Q = Q or self.SP
        deps = self._deps(reads, writes)
        if sb.dsem is None:
            sb.dsem = self.new_sem()
            self.dbufs.append(sb)
        if sb.dcnt > 0:
            _merge(deps, {id(sb.dsem): (sb.dsem, sb.dcnt * 16)})
        self._wait(Q, deps)
        sb.dcnt += 1
        tk = (sb.dsem, sb.dcnt * 16)
        Q.ops.append(("op", "dma_start", (), dict(out=out, in_=in_), sb.dsem, 16))
        self._commit(tk, reads, writes)

    def switch(self, old, new):
        deps = {}
        for b in old:
            _merge(deps, b.w)
            _merge(deps, b.r)
        for b in new:
            d = dict(deps)
            _merge(d, b.w)
            _merge(d, b.r)
            b.w = d
            b.r = {}

    def barrier(self):
        deps = {}
        for E in self.engs:
            if E.count > 0:
                deps[id(E.sem)] = (E.sem, E.count)
        for b in self.dbufs:
            deps[id(b.dsem)] = (b.dsem, b.dcnt * 16)
        for E in self.engs:
            self._wait(E, deps)

    def replay(self, E, e):
        for o in E.ops:
            if o[0] == "wait":
                e.wait_ge(o[1], o[2])
            else:
                _, m, a, k, sem, inc = o
                ins = getattr(e, m)(*a, **k)
                if sem is not None:
                    ins.then_inc(sem, inc)


def build(NG):
    L = 1024 * NG
    NCH = 2 * NG
    nc = bass.Bass("TRN2", target_bir_lowering=False)

    def din(name, shape, dt=F32):
        return nc.dram_tensor(name, list(shape), dt, kind="ExternalInput").ap()

    x_seq = din("x_seq", [L, 1024])
    x_own = din("x_own", [NG, 544, 1024])
    mem = din("mem", [256, 1024])
    cmask = din("cmask", [4, 128, 1024])
    vecs_d = din("vecs", [128, 72])
    cw_d = din("cwfm", [128, 248])
    gfin_d = din("gfin", [1024])
    ident_d = din("ident", [128, 128])
    pow2_d = din("pow2", [128, NIT])
    w_in = din("w_in", [1024, 7752])
    w_co = din("w_conv_out", [1024, 1024])
    w_ao = din("w_attn_out", [1024, 1024])
    w_mix = din("w_mix_out", [1024, 1024])
    wx_q = din("wx_q", [1024, 512])
    wx_kv = din("wx_kv", [1024, 1024])
    wx_o = din("wx_o", [512, 1024])
    w_ff1 = din("w_ff1", [1024, 4096])
    w_ff2 = din("w_ff2", [4096, 1024])
    out = nc.dram_tensor("out", [NG * 512, 1024], F32, kind="ExternalOutput").ap()
    Wscr = nc.dram_tensor("wscr", [NUNITS, 128, 8, 512], BF16, kind="Internal").ap()
    Kt = nc.dram_tensor("ktscr", [8, 128, L], BF16, kind="Internal").ap()
    Vs = nc.dram_tensor("vscr", [L, 1024], BF16, kind="Internal").ap()

    with contextlib.ExitStack() as st:
        kb = KB(nc, st)
        PE, ACT, DVE, POOL, SP = kb.PE, kb.ACT, kb.DVE, kb.POOL, kb.SP

        def sb(name, shape, dt):
            return st.enter_context(nc.sbuf_tensor("sb_" + name, list(shape), dt))

        kiT = sb("kiT", [128, L], BF16)
        ident = sb("ident", [128, 128], BF16)
        ones = sb("ones", [128, 128], BF16)
        gfin = sb("gfin", [128, 1024], F32)
        vecs = sb("vecs", [128, 72], F32)
        cw = sb("cw", [128, 248], F32)
        wwi = sb("wwi", [128, 8, 8], BF16)
        KmT = sb("KmT", [128, 4, 256], BF16)
        Vm = sb("Vm", [128, 2, 512], BF16)
        pow2 = sb("pow2", [128, NIT], F32)
        small = sb("small", [128, 96], F32)
        xt = sb("xt", [128, 4, 1024], F32)
        xh = sb("xh", [32, 1024], F32)
        m1 = sb("m1", [128, 8, 512], BF16)
        qiT = sb("qiT", [128, 4, 512], BF16)
        wi = sb("wi", [128, 4, 8], F32)
        nT = sb("nT", [128, 8, 512], BF16)
        nTh = sb("nTh", [128, 8, 32], BF16)
        xn = sb("xn", [128, 2, 1024], BF16)
        B1 = sb("B1", [128, 8, 544], BF16)
        B2 = sb("B2", [128, 8, 512], BF16)
        wp = sb("wp", [128, 3, 8 * 512], BF16)
        A1 = sb("A1", [128, 8192], F32)
        A2 = sb("A2", [128, 8192], F32)
        A3 = sb("A3", [128, 6144], F32)
        psum = [st.enter_context(nc.psum_tensor("ps%d" % i, [128, 512], F32)) for i in range(8)]

        def view(ar, off, dt, shape):
            sz = {F32: 4, BF16: 2, FP8: 1}[dt]
            n = int(np.prod(shape)) * sz
            assert off % 4 == 0 and n % 4 == 0
            ap = ar[:, off // 4:(off + n) // 4]
            if dt != F32:
                ap = ap.bitcast(dt)
            if len(shape) == 2:
                ap = ap.rearrange("p (a b) -> p a b", b=shape[1])
            return ap

        def bl(name, n):
            return [Buf("%s%d" % (name, i)) for i in range(n)]

        b_kiT = bl("kiT", NCH)
        b_const = Buf("const")
        b_ident, b_ones, b_gfin, b_vecs, b_cw, b_wwi, b_KmT, b_Vm, b_pow2 = [Buf(n) for n in
            ("ident", "ones", "gfin", "vecs", "cw", "wwi", "KmT", "Vm", "pow2")]
        b_xt = bl("xt", 4)
        b_xh = Buf("xh")
        b_m1 = bl("m1", 8)
        b_qiT = bl("qiT", 4)
        b_wi = bl("wi", 4)
        b_nT = bl("nT", 4)
        b_nTh = Buf("nTh")
        b_xn = bl("xn", 2)
        b_B1 = bl("B1", 8)
        b_B2 = bl("B2", 8)
        b_wp = bl("wp", 3)
        b_ps = bl("ps", 8)
        b_wscr = bl("wscr", NUNITS)
        b_kt = [bl("kt%d_" % c, 8) for c in range(NCH)]
        b_vs = [bl("vs%d_" % c, 4) for c in range(NCH)]
        b_out = Buf("out")
        sm_names = ["ss", "std", "rstd", "lo", "hi", "rng", "mid", "ctot", "delta", "eps", "nmid", "sA"]
        sm = {n: (small[:, i:i + 1], Buf("sm_" + n)) for i, n in enumerate(sm_names)}
        cnt_ap, b_cnt = small[:, 16:24], Buf("cnt")
        cntA_ap, b_cntA = small[:, 24:32], Buf("cntA")
        st_ap, b_st = small[:, 32:32 + NIT], Buf("st")
        ssr = [(small[:, 60 + i:61 + i], Buf("ssr%d" % i)) for i in range(4)]
        rsr = [(small[:, 64 + i:65 + i], Buf("rsr%d" % i)) for i in range(4)]

        pinned = set()
        rr = [0]

        def bank():
            for _ in range(16):
                i = rr[0] % 8
                rr[0] += 1
                if i not in pinned:
                    return i
            raise RuntimeError("no psum bank")

        def pin():
            i = bank()
            pinned.add(i)
            return i

        def unpin(i):
            pinned.discard(i)

        engrr = [0]

        def ev_eng():
            engrr[0] += 1
            return ACT if engrr[0] % 2 else DVE

        def copy_op(E, out_ap, in_ap, reads, writes, scale=None):
            if E is ACT:
                kw = {} if scale is None else dict(scale=scale)
                kb.op(ACT, "activation", (out_ap, in_ap, AF.Copy), kw, reads, writes)
            else:
                if scale is None:
                    kb.op(E, "tensor_copy", (out_ap, in_ap), {}, reads, writes)
                else:
                    kb.op(E, "tensor_scalar", (out_ap, in_ap, float(scale), None, ALU.mult), {}, reads, writes)

        kb.dma(vecs[:], vecs_d, b_vecs, writes=[b_vecs])
        kb.dma(cw[:], cw_d, b_cw, writes=[b_cw])
        kb.dma(gfin[:], gfin_d.partition_broadcast(128), b_gfin, writes=[b_gfin])
        kb.dma(pow2[:], pow2_d, b_pow2, writes=[b_pow2])
        idst = view(A3, 0, F32, [128])
        b_idst = Buf("idst")
        kb.dma(idst, ident_d, b_idst, writes=[b_idst])
        kb.op(DVE, "tensor_copy", (ident[:], idst), {}, [b_idst], [b_ident])
        kb.op(DVE, "memset", (ones[:], 1.0), {}, [], [b_ones])
        kb.op(DVE, "memset", (sm["eps"][0], EPS), {}, [], [sm["eps"][1]])

        stage_in = [view(A2, i * 8192, F32, [8, 256]) for i in range(2)]
        stage_out = [view(A2, 16384 + i * 4096, BF16, [8, 256]) for i in range(2)]
        b_si = bl("si", 2)
        b_so = bl("so", 2)
        Wki = view(A2, 24576, BF16, [8, 128])
        b_Wki = Buf("Wki")
        prep_i = [0]

        def prep(src, r0, nkc, c0, ncol, gcol, dst_fn, dst_reads_writes):
            k = prep_i[0] % 2
            prep_i[0] += 1
            si, so = stage_in[k], stage_out[k]
            kb.dma(si[:, :nkc, :ncol], src[r0:r0 + nkc * 128, c0:c0 + ncol].rearrange("(kc p) c -> p kc c", p=128),
                   b_si[k], writes=[b_si[k]])
            dst, dbufs, is_dram = dst_fn()
            tgt = so[:, :nkc, :ncol] if is_dram else dst
            wr = [b_so[k]] if is_dram else dbufs
            if gcol is None:
                E = POOL if prep_i[0] % 3 else DVE
                kb.op(E, "tensor_copy", (tgt, si[:, :nkc, :ncol]), {}, [b_si[k]], wr)
            else:
                E = DVE if prep_i[0] % 2 else ACT
                insts = []
                for kc in range(nkc):
                    if E is ACT:
                        insts.append(("activation", (tgt[:, kc, :], si[:, kc, :ncol], AF.Copy), dict(scale=vecs[:, gcol + kc:gcol + kc + 1])))
                    else:
                        insts.append(("tensor_scalar", (tgt[:, kc, :], si[:, kc, :ncol], vecs[:, gcol + kc:gcol + kc + 1], None, ALU.mult), {}))
                kb.group(E, insts, [b_si[k], b_vecs], wr)
            if is_dram:
                kb.dma(dst, so[:, :nkc, :ncol], b_so[k], reads=[b_so[k]], writes=dbufs)

        def prep_unit(src, r0, nkc, c0, gcol, uid):
            for h in range(2):
                prep(src, r0, nkc, c0 + h * 256, 256, gcol,
                     lambda h=h: (Wscr[uid][:, :nkc, h * 256:(h + 1) * 256], [b_wscr[uid]], True), None)

        def prep_res(src, c0, ncols, gcol, tile_ap, buf, dcol0=0):
            for cc in range(0, ncols, 256):
                n = min(256, ncols - cc)
                prep(src, 0, 8, c0 + cc, n, gcol,
                     lambda cc=cc, n=n: (tile_ap[:, :, dcol0 + cc:dcol0 + cc + n], [buf], False), None)

        kb.stage = 'setup'
        wkv = view(A1, 0, BF16, [8, 1024])
        b_wkv = Buf("wkv")
        memT = view(A1, 16384, BF16, [8, 256])
        b_memT = Buf("memT")
        prep_res(wx_kv, 0, 1024, V_GMEM, wkv, b_wkv)

        tpi = [0]

        def norm_T(x_ap, xbuf, np_, dst_ap, dst_bufs, nrm_only=False):
            k = tpi[0] % 2
            tpi[0] += 1
            q = tpi[0] % 4
            ss_ap, ss_b = ssr[q]
            rs_ap, rs_b = rsr[q]
            xn_ap = xn[:np_, k, :]
            kb.op(ACT, "activation", (xn_ap, x_ap, AF.Square), dict(accum_out=ss_ap[:np_]), [xbuf], [b_xn[k], ss_b])
            kb.op(ACT, "activation", (rs_ap[:np_], ss_ap[:np_], AF.Sqrt), dict(scale=1.0 / 1024, bias=sm["eps"][0][:np_]),
                  [ss_b, sm["eps"][1]], [rs_b])
            kb.op(DVE, "reciprocal", (rs_ap[:np_], rs_ap[:np_]), {}, [rs_b], [rs_b])
            if nrm_only:
                return rs_ap, rs_b
            kb.op(ACT, "activation", (xn_ap, x_ap, AF.Copy), dict(scale=rs_ap[:np_]), [xbuf, rs_b], [b_xn[k]])
            bi = bank()
            pb = psum[bi][:].bitcast(BF16)
            insts = []
            for kc in range(8):
                insts.append(("transpose", (pb[:, kc * np_:(kc + 1) * np_], xn[:np_, k, kc * 128:(kc + 1) * 128], ident[:np_, :np_]), {}))
            kb.group(PE, insts, [b_xn[k], b_ident], [b_ps[bi]])
            E = ev_eng()
            copy_op(E, dst_ap, pb[:, :8 * np_].rearrange("p (k t) -> p k t", t=np_), [b_ps[bi]], dst_bufs)

        def mm_fm(bi, lhs_fn, rhs_fn, nkc, ncols, reads, N=512):
            insts = []
            for kc in range(nkc):
                insts.append(("matmul", (psum[bi][:ncols, :N], lhs_fn(kc), rhs_fn(kc)), dict(start=(kc == 0), stop=(kc == nkc - 1))))
            kb.group(PE, insts, reads, [b_ps[bi]])

        for mt in range(2):
            kb.dma(xt[:, mt, :], mem[mt * 128:(mt + 1) * 128, :], b_xt[mt], writes=[b_xt[mt]])
            norm_T(xt[:, mt, :], b_xt[mt], 128, memT[:, :, mt * 128:(mt + 1) * 128], [b_memT])
        for xhd in range(4):
            bi = bank()
            mm_fm(bi, lambda kc: wkv[:, kc, xhd * 128:(xhd + 1) * 128], lambda kc: memT[:, kc, :], 8, 128, [b_wkv, b_memT], N=256)
            copy_op(ev_eng(), KmT[:, xhd, :], psum[bi][:, :256], [b_ps[bi]], [b_KmT])
        for mt in range(2):
            bi = bank()
            mm_fm(bi, lambda kc: memT[:, kc, mt * 128:(mt + 1) * 128], lambda kc: wkv[:, kc, 512:1024], 8, 128, [b_wkv, b_memT])
            copy_op(ev_eng(), Vm[:, mt, :], psum[bi][:, :], [b_ps[bi]], [b_Vm])

        Wk = view(A1, 0, BF16, [8, 1024])
        Wv = view(A1, 16384, BF16, [8, 1024])
        b_Wk, b_Wv = Buf("Wk"), Buf("Wv")
        kb.switch([b_wkv, b_memT], [b_Wk, b_Wv])
        prep_res(w_in, 3072, 1024, V_GMIX, Wk, b_Wk)
        prep_res(w_in, 4096, 1024, V_GMIX, Wv, b_Wv)
        prep_res(w_in, 5632, 64, V_GMIX, Wki, b_Wki, 0)
        prep_res(w_in, 5632, 64, V_GMIX, Wki, b_Wki, 64)
        prep_res(w_in, 5696, 8, V_GMIX, wwi, b_wwi)

        def rest_jobs():
            jobs = []
            J = lambda *a: jobs.append(a)
            for u in range(2):
                J(w_in, 0, 8, 0 + u * 512, V_GMIX, U_CA + u)
                J(w_in, 0, 8, 1024 + u * 512, V_GMIX, U_CG + u)
            for u in range(2):
                J(w_in, 0, 8, 5704 + u * 512, V_GMIX, U_GC + u)
                J(w_co, 0, 8, u * 512, None, U_CO + u)
            J(w_in, 0, 8, 5120, V_GMIX, U_QI)
            for u in range(2):
                J(w_in, 0, 8, 2048 + u * 512, V_GMIX, U_Q + u)
                J(w_in, 0, 8, 6728 + u * 512, V_GMIX, U_GA + u)
                J(w_ao, 0, 8, u * 512, None, U_AO + u)
                J(w_mix, 0, 8, u * 512, None, U_MIX + u)
            J(wx_q, 0, 8, 0, V_GX, U_XQ)
            for u in range(2):
                J(wx_o, 0, 4, u * 512, None, U_XO + u)
            for u in range(8):
                J(w_ff1, 0, 8, u * 512, V_GFFN, U_FF1 + u)
            for ch in range(2):
                for kg in range(4):
                    J(w_ff2, kg * 1024, 8, ch * 512, None, U_FF2 + ch * 4 + kg)
            return jobs

        kb.stage = 'phaseA'
        kst = [view(A3, i * 1024, BF16, [512]) for i in range(2)]
        vst = [view(A3, 2048 + i * 2048, BF16, [1024]) for i in range(2)]
        b_kst, b_vst = bl("kst", 2), bl("vst", 2)
        kb.switch([b_idst], b_kst + b_vst)
        ksi = [0]
        pjobs = rest_jobs()
        per_chunk = -(-len(pjobs) // NCH)
        for tt in range(4):
            kb.dma(xt[:, tt, :], x_seq[tt * 128:(tt + 1) * 128, :], b_xt[tt], writes=[b_xt[tt]])
        for cc in range(NCH):
            for tt in range(4):
                norm_T(xt[:, tt, :], b_xt[tt], 128, nT[:, :, tt * 128:(tt + 1) * 128], [b_nT[tt]])
                if cc + 1 < NCH:
                    kb.dma(xt[:, tt, :], x_seq[(cc + 1) * 512 + tt * 128: (cc + 1) * 512 + (tt + 1) * 128, :], b_xt[tt], writes=[b_xt[tt]])
            for h in range(8):
                bi = bank()
                mm_fm(bi, lambda kc: Wk[:, kc, h * 128:(h + 1) * 128], lambda kc: nT[:, kc, :], 8, 128, [b_Wk] + b_nT)
                k = ksi[0] % 2
                ksi[0] += 1
                copy_op(ev_eng(), kst[k], psum[bi][:, :], [b_ps[bi]], [b_kst[k]])
                kb.dma(Kt[h][:, cc * 512:(cc + 1) * 512], kst[k], b_kst[k], reads=[b_kst[k]], writes=[b_kt[cc][h]])
            for tt in range(4):
                k = (cc * 4 + tt) % 2
                for ch in range(2):
                    bi = bank()
                    mm_fm(bi, lambda kc: nT[:, kc, tt * 128:(tt + 1) * 128], lambda kc: Wv[:, kc, ch * 512:(ch + 1) * 512], 8, 128,
                          [b_Wv, b_nT[tt]])
                    copy_op(ev_eng(), vst[k][:, ch * 512:(ch + 1) * 512], psum[bi][:, :], [b_ps[bi]], [b_vst[k]])
                kb.dma(Vs[cc * 512 + tt * 128: cc * 512 + (tt + 1) * 128, :], vst[k], b_vst[k], reads=[b_vst[k]], writes=[b_vs[cc][tt]])
            bi = bank()
            mm_fm(bi, lambda kc: Wki[:, kc, :], lambda kc: nT[:, kc, :], 8, 128, [b_Wki] + b_nT)
            copy_op(ev_eng(), kiT[:, cc * 512:(cc + 1) * 512], psum[bi][:, :], [b_ps[bi]], [b_kiT[cc]])
            kb.stage = 'prep'
            for _ in range(per_chunk):
                if pjobs:
                    prep_unit(*pjobs.pop(0))
            kb.stage = 'phaseA'
        while pjobs:
            prep_unit(*pjobs.pop(0))

        wpi = [0]

        def wload(uid, nkc=8):
            k = wpi[0] % 3
            wpi[0] += 1
            w = wp[:, k, :].rearrange("p (a b) -> p a b", b=512)
            kb.dma(w[:, :nkc, :], Wscr[uid][:, :nkc, :], b_wp[k], reads=[b_wscr[uid]], writes=[b_wp[k]])
            return w, b_wp[k]

        Dg = view(A1, 0, BF16, [31, 128])
        sq = [view(A1, 7936 + i * 1024, BF16, [512]) for i in range(2)]
        gtmp = [view(A1, 9984 + i * 1024, BF16, [512]) for i in range(2)]
        meanB = view(A1, 12032, F32, [512])
        rstdB = view(A1, 14080, F32, [512])
        msq = view(A1, 16128, F32, [512])
        lnt = [view(A1, 18176 + i * 2048, F32, [512]) for i in range(2)]
        b_Dg, b_sq, b_gtmp, b_meanB, b_rstdB, b_msq, b_lnt = Buf("Dg"), bl("sq", 2), bl("gtmp", 2), Buf("meanB"), Buf("rstdB"), Buf("msq"), bl("lnt", 2)
        a1_conv = [b_Dg, b_meanB, b_rstdB, b_msq] + b_sq + b_gtmp + b_lnt
        NKB = L // 128
        maskT = view(A1, 0, FP8, [NKB, 512])
        b_mask = bl("mask", NKB // 8)
        score = A2[:, :L]
        b_score = bl("score", NCH)
        aT = view(A2, 0, BF16, [32, 512])
        b_aT = bl("aT", 32)
        rt = [view(A3, i * 2048, F32, [512]) for i in range(3)]
        mp = [view(A3, 6144 + i * 2048, BF16, [1024]) for i in range(2)]
        junk = view(A3, 10240, FP8, [1024])
        cm = view(A3, 11264, F32, [1024])
        junk2 = view(A3, 15360, FP8, [1024])
        b_junk2 = Buf("junk2")
        b_rt, b_mp, b_junk, b_cm = bl("rt", 3), bl("mp", 2), Buf("junk"), Buf("cm")
        a3_idx = b_rt + b_mp + [b_junk, b_cm, b_junk2]
        Kbuf = [view(A3, i * 2048, BF16, [1024]) for i in range(3)]
        Vbuf = [view(A3, 6144 + i * 2048, BF16, [8, 128]) for i in range(3)]
        qTb = [view(A3, 12288 + i * 1024, BF16, [512]) for i in range(2)]
        Pb = [view(A3, 14336 + i * 1024, BF16, [512]) for i in range(6)]
        rz = view(A3, 20480, F32, [512])
        b_Kbuf, b_Vbuf, b_qTb, b_Pb, b_rz = bl("Kbuf", 3), bl("Vbuf", 3), bl("qTb", 2), bl("Pb", 6), Buf("rz")
        a3_attn = b_Kbuf + b_Vbuf + b_qTb + b_Pb + [b_rz]
        a1_prev = [b_Wk, b_Wv]
        a2_prev = b_si + b_so + [b_Wki]
        a3_prev = b_kst + b_vst

        poolrr = [0]

        def mul_eng():
            poolrr[0] += 1
            return POOL if poolrr[0] % 2 == 0 else DVE

        def resid_add(tt, ch, bi):
            kb.op(DVE, "tensor_tensor", (xt[:, tt, ch * 512:(ch + 1) * 512], xt[:, tt, ch * 512:(ch + 1) * 512], psum[bi][:, :], ALU.add),
                  {}, [b_ps[bi], b_xt[tt]], [b_xt[tt]])

        def attn_core(n_kt, k_fn, v_fn, mask_fn, qT_ap, qT_b, dst_ap, dst_b, kv_reads_fn):
            bo, bz = pin(), pin()
            pend = []

            def flush(item, first, last):
                t, pk = item
                kb.op(PE, "matmul", (psum[bo][:, :], v_fn(t), Pb[pk]), dict(start=first, stop=last), kv_reads_fn(t) + [b_Pb[pk]], [b_ps[bo]])
                kb.op(PE, "matmul", (psum[bz][:, :], ones[:], Pb[pk]), dict(start=first, stop=last), [b_ones, b_Pb[pk]], [b_ps[bz]])

            done = 0
            for t in range(n_kt):
                bs = bank()
                kb.op(PE, "matmul", (psum[bs][:, :], k_fn(t), qT_ap), dict(start=True, stop=True), kv_reads_fn(t) + [qT_b], [b_ps[bs]])
                ek = t % 6
                mfn = mask_fn(t) if mask_fn is not None else None
                kb.op(ACT, "activation", (Pb[ek], psum[bs][:, :], AF.Exp), {}, [b_ps[bs]], [b_Pb[ek]])
                if mfn is not None:
                    kb.op(DVE, "tensor_tensor", (Pb[ek], Pb[ek], mfn[0], ALU.mult), {}, [b_Pb[ek], mfn[1]], [b_Pb[ek]])
                pend.append((t, ek))
                if len(pend) > 3:
                    flush(pend.pop(0), done == 0, False)
                    done += 1
            while pend:
                it = pend.pop(0)
                flush(it, done == 0, len(pend) == 0)
                done += 1
            kb.op(DVE, "reciprocal", (rz, psum[bz][:, :]), {}, [b_ps[bz]], [b_rz])
            kb.op(DVE, "tensor_tensor", (dst_ap, psum[bo][:, :], rz, ALU.mult), {}, [b_ps[bo], b_rz], [dst_b])
            unpin(bo)
            unpin(bz)

        for g in range(NG):
            N = (2 * g + 2) * 512
            nch = N // 512
            nkb = N // 128
            kb.stage = 'g%d.G1' % g
            kb.dma(xh[:, :], x_own[g, 0:32, :], b_xh, writes=[b_xh])
            for tt in range(4):
                kb.dma(xt[:, tt, :], x_own[g, 32 + tt * 128: 32 + (tt + 1) * 128, :], b_xt[tt], writes=[b_xt[tt]])
            norm_T(xh[:, :], b_xh, 32, nTh[:, :, :], [b_nTh])
            for tt in range(4):
                norm_T(xt[:, tt, :], b_xt[tt], 128, nT[:, :, tt * 128:(tt + 1) * 128], [b_nT[tt]])

            kb.stage = 'g%d.G2' % g
            kb.switch(a1_prev, a1_conv)
            a1_prev = a1_conv
            for u in range(2):
                wa, bwa = wload(U_CA + u)
                wg, bwg = wload(U_CG + u)
                for j in range(4):
                    ct = u * 4 + j
                    ba, bg, bh = bank(), bank(), bank()
                    mm_fm(ba, lambda kc: wa[:, kc, j * 128:(j + 1) * 128], lambda kc: nT[:, kc, :], 8, 128, [bwa] + b_nT)
                    mm_fm(bg, lambda kc: wg[:, kc, j * 128:(j + 1) * 128], lambda kc: nT[:, kc, :], 8, 128, [bwg] + b_nT)
                    insts = []
                    for kc in range(8):
                        insts.append(("matmul", (psum[bh][:, 0:32], wa[:, kc, j * 128:(j + 1) * 128], nTh[:, kc, :]), dict(start=(kc == 0), stop=(kc == 7))))
                    for kc in range(8):
                        insts.append(("matmul", (psum[bh][:, 32:64], wg[:, kc, j * 128:(j + 1) * 128], nTh[:, kc, :]), dict(start=(kc == 0), stop=(kc == 7))))
                    kb.group(PE, insts, [bwa, bwg, b_nTh], [b_ps[bh]])
                    k = ct % 2
                    kb.op(ACT, "activation", (gtmp[k], psum[bg][:, :], AF.Sigmoid), {}, [b_ps[bg]], [b_gtmp[k]])
                    kb.op(DVE, "tensor_tensor", (B1[:, ct, 32:544], psum[ba][:, :], gtmp[k], ALU.mult), {}, [b_ps[ba], b_gtmp[k]], [b_B1[ct]])
                    kb.op(ACT, "activation", (lnt[k][:, 0:32], psum[bh][:, 32:64], AF.Sigmoid), {}, [b_ps[bh]], [b_lnt[k]])
                    kb.op(DVE, "tensor_tensor", (B1[:, ct, 0:32], psum[bh][:, 0:32], lnt[k][:, 0:32], ALU.mult), {}, [b_ps[bh], b_lnt[k]], [b_B1[ct]])
            s1, s2 = pin(), pin()
            for ct in range(8):
                kb.op(DVE, "tensor_tensor", (Dg, ident[:].unsqueeze(1).to_broadcast([128, 31, 128]),
                                             cw[:, ct * 31:(ct + 1) * 31].unsqueeze(2).to_broadcast([128, 31, 128]), ALU.mult), {},
                      [b_ident, b_cw], [b_Dg])
                bc = bank()
                insts = []
                for k in range(31):
                    insts.append(("matmul", (psum[bc][:, :], Dg[:, k, :], B1[:, ct, 2 + k: 2 + k + 512]), dict(start=(k == 0), stop=(k == 30))))
                kb.group(PE, insts, [b_Dg, b_B1[ct]], [b_ps[bc]])
                k2 = ct % 2
                kb.op(ACT, "activation", (B2[:, ct, :], psum[bc][:, :], AF.Identity), dict(bias=vecs[:, V_CB + ct:V_CB + ct + 1]),
                      [b_ps[bc], b_vecs], [b_B2[ct]])
                kb.op(ACT, "activation", (sq[k2], psum[bc][:, :], AF.Square), dict(bias=vecs[:, V_CB + ct:V_CB + ct + 1]),
                      [b_ps[bc], b_vecs], [b_sq[k2]])
                kb.op(PE, "matmul", (psum[s1][:, :], ones[:], B2[:, ct, :]), dict(start=(ct == 0), stop=(ct == 7)), [b_ones, b_B2[ct]], [b_ps[s1]])
                kb.op(PE, "matmul", (psum[s2][:, :], ones[:], sq[k2]), dict(start=(ct == 0), stop=(ct == 7)), [b_ones, b_sq[k2]], [b_ps[s2]])
            kb.op(ACT, "activation", (meanB, psum[s1][:, :], AF.Copy), dict(scale=1.0 / 1024), [b_ps[s1]], [b_meanB])
            kb.op(DVE, "tensor_tensor", (msq, meanB, meanB, ALU.mult), {}, [b_meanB], [b_msq])
            kb.op(DVE, "scalar_tensor_tensor", (rstdB, psum[s2][:, :], 1.0 / 1024, msq, ALU.mult, ALU.subtract), {}, [b_ps[s2], b_msq], [b_rstdB])
            kb.op(ACT, "activation", (rstdB, rstdB, AF.Sqrt), dict(bias=sm["eps"][0]), [b_rstdB, sm["eps"][1]], [b_rstdB])
            kb.op(DVE, "reciprocal", (rstdB, rstdB), {}, [b_rstdB], [b_rstdB])
            unpin(s1)
            unpin(s2)
            for ct in range(8):
                k = ct % 2
                kb.op(DVE, "tensor_tensor", (lnt[k], B2[:, ct, :], meanB, ALU.subtract), {}, [b_B2[ct], b_meanB], [b_lnt[k]])
                kb.op(POOL, "tensor_tensor", (lnt[k], lnt[k], rstdB, ALU.mult), {}, [b_lnt[k], b_rstdB], [b_lnt[k]])
                kb.op(ACT, "activation", (B2[:, ct, :], lnt[k], AF.Silu),
                      dict(scale=vecs[:, V_LG + ct:V_LG + ct + 1], bias=vecs[:, V_LB + ct:V_LB + ct + 1]), [b_lnt[k], b_vecs], [b_B2[ct]])
            for u in range(2):
                wc, bwc = wload(U_CO + u)
                wg, bwg = wload(U_GC + u)
                for j in range(4):
                    ft = u * 4 + j
                    by, bg = bank(), bank()
                    mm_fm(by, lambda kc: wc[:, kc, j * 128:(j + 1) * 128], lambda kc: B2[:, kc, :], 8, 128, [bwc] + b_B2)
                    mm_fm(bg, lambda kc: wg[:, kc, j * 128:(j + 1) * 128], lambda kc: nT[:, kc, :], 8, 128, [bwg] + b_nT)
                    k = ft % 2
                    kb.op(ACT, "activation", (gtmp[k], psum[bg][:, :], AF.Sigmoid), dict(bias=vecs[:, V_BGC + ft:V_BGC + ft + 1]),
                          [b_ps[bg], b_vecs], [b_gtmp[k]])
                    kb.op(DVE, "tensor_tensor", (m1[:, ft, :], psum[by][:, :], gtmp[k], ALU.mult), {}, [b_ps[by], b_gtmp[k]], [b_m1[ft]])

            kb.stage = 'g%d.G3' % g
            wq_, bwq = wload(U_QI)
            for m in range(4):
                bi = bank()
                mm_fm(bi, lambda kc: wq_[:, kc, m * 128:(m + 1) * 128], lambda kc: nT[:, kc, :], 8, 128, [bwq] + b_nT)
                copy_op(ev_eng(), qiT[:, m, :], psum[bi][:, :], [b_ps[bi]], [b_qiT[m]], scale=0.125)
            for i in range(4):
                bi = bank()
                insts = []
                for kc in range(8):
                    insts.append(("matmul", (psum[bi][:, 0:8], nT[:, kc, i * 128:(i + 1) * 128], wwi[:, kc, :]), dict(start=(kc == 0), stop=(kc == 7))))
                kb.group(PE, insts, [b_nT[i], b_wwi], [b_ps[bi]])
                copy_op(ACT, wi[:, i, :], psum[bi][:, 0:8], [b_ps[bi]], [b_wi[i]], scale=8.0 ** -0.5)
            kb.switch(a2_prev, b_score)
            a2_prev = b_score
            kb.switch(a1_prev, b_mask)
            a1_prev = b_mask
            kb.switch(a3_prev, a3_idx)
            a3_prev = a3_idx
            rti = [0]
            for i in range(4):
                kb.dma(cm, cmask[i], b_cm, writes=[b_cm])
                for c5 in range(nch):
                    for h in range(8):
                        m, s = h // 2, h % 2
                        bi = bank()
                        kb.op(PE, "matmul", (psum[bi][:, :], qiT[s * 64:(s + 1) * 64, m, i * 128:(i + 1) * 128],
                                             kiT[s * 64:(s + 1) * 64, c5 * 512:(c5 + 1) * 512]), dict(start=True, stop=True),
                              [b_qiT[m], b_kiT[c5]], [b_ps[bi]])
                        k = rti[0] % 3
                        rti[0] += 1
                        kb.op(ACT, "activation", (rt[k], psum[bi][:, :], AF.Relu), {}, [b_ps[bi]], [b_rt[k]])
                        sc = score[:, c5 * 512:(c5 + 1) * 512]
                        if h == 0:
                            kb.op(DVE, "tensor_scalar", (sc, rt[k], wi[:, i, 0:1], None, ALU.mult), {}, [b_rt[k], b_wi[i]], [b_score[c5]])
                        else:
                            kb.op(DVE, "scalar_tensor_tensor", (sc, rt[k], wi[:, i, h:h + 1], sc, ALU.mult, ALU.add), {},
                                  [b_rt[k], b_wi[i], b_score[c5]], [b_score[c5]])
                lo, b_lo = sm["lo"]
                hi, b_hi = sm["hi"]
                rng, b_rng = sm["rng"]
                mid, b_mid = sm["mid"]
                nmid, b_nmid = sm["nmid"]
                sA, b_sA = sm["sA"]
                ctot, b_ctot = sm["ctot"]
                delta, b_delta = sm["delta"]
                sc_all = b_score[:nch]
                kb.op(DVE, "tensor_reduce", (lo, score[:, :N], AX.X, ALU.min), {}, sc_all, [b_lo])
                kb.op(DVE, "tensor_tensor", (score[:, N - 1024:N], score[:, N - 1024:N], cm, ALU.add), {},
                      [b_cm, b_score[nch - 2], b_score[nch - 1]], [b_score[nch - 2], b_score[nch - 1]])
                kb.op(DVE, "tensor_reduce", (hi, score[:, :N], AX.X, ALU.max), {}, sc_all, [b_hi])
                kb.op(DVE, "tensor_tensor", (rng, hi, lo, ALU.subtract), {}, [b_hi, b_lo], [b_rng])
                kb.op(DVE, "tensor_scalar", (st_ap, pow2[:, :], rng, None, ALU.mult), {}, [b_pow2, b_rng], [b_st])
                npc = N // 1024
                a_pcs = [pc for pc in range(npc) if pc % 2 == 0]
                d_pcs = [pc for pc in range(npc) if pc % 2 == 1]
                thr2 = 2.0 * (float(KTOP) - 0.5) - 1024.0 * len(a_pcs)
                for it in range(NIT):
                    kb.op(DVE, "tensor_tensor", (mid, lo, st_ap[:, it:it + 1], ALU.add), {}, [b_lo, b_st], [b_mid])
                    for j, pc in enumerate(a_pcs):
                        kb.op(ACT, "activation", (junk2, score[:, pc * 1024:(pc + 1) * 1024], AF.Sign),
                              dict(scale=-1.0, bias=mid, accum_out=cntA_ap[:, j:j + 1], saturate=False),
                              [b_score[2 * pc], b_score[2 * pc + 1], b_mid], [b_junk2, b_cntA])
                    for j, pc in enumerate(d_pcs):
                        kb.op(DVE, "tensor_scalar", (junk, score[:, pc * 1024:(pc + 1) * 1024], mid, 0.0, ALU.is_ge, ALU.add),
                              dict(accum_out=cnt_ap[:, j:j + 1], saturate=False), [b_score[2 * pc], b_score[2 * pc + 1], b_mid], [b_junk, b_cnt])
                    kb.op(DVE, "tensor_reduce", (sA, cntA_ap[:, :len(a_pcs)], AX.X, ALU.add), {}, [b_cntA], [b_sA])
                    if d_pcs:
                        kb.op(DVE, "tensor_reduce", (ctot, cnt_ap[:, :len(d_pcs)], AX.X, ALU.add), {}, [b_cnt], [b_ctot])
                        kb.op(DVE, "scalar_tensor_tensor", (sA, ctot, 2.0, sA, ALU.mult, ALU.subtract), {}, [b_ctot, b_sA], [b_sA])
                        kb.op(DVE, "scalar_tensor_tensor", (delta, sA, thr2, st_ap[:, it:it + 1], ALU.is_ge, ALU.mult), {},
                              [b_sA, b_st], [b_delta])
                    else:
                        kb.op(DVE, "scalar_tensor_tensor", (delta, sA, -thr2, st_ap[:, it:it + 1], ALU.is_le, ALU.mult), {},
                              [b_sA, b_st], [b_delta])
                    kb.op(DVE, "tensor_tensor", (lo, lo, delta, ALU.add), {}, [b_lo, b_delta], [b_lo])
                for pc in range(npc):
                    k = pc % 2
                    kb.op(DVE, "tensor_scalar", (mp[k], score[:, pc * 1024:(pc + 1) * 1024], lo, None, ALU.is_ge), {},
                          [b_score[2 * pc], b_score[2 * pc + 1], b_lo], [b_mp[k]])
                    bi = bank()
                    pb = psum[bi][:].bitcast(BF16)
                    insts = []
                    for j in range(8):
                        insts.append(("transpose", (pb[:, j * 128:(j + 1) * 128], mp[k][:, j * 128:(j + 1) * 128], ident[:]), {}))
                    kb.group(PE, insts, [b_mp[k], b_ident], [b_ps[bi]])
                    kb.op(ACT, "activation", (maskT[:, pc * 8:(pc + 1) * 8, i * 128:(i + 1) * 128], pb.rearrange("p (k q) -> p k q", q=128), AF.Copy),
                          dict(saturate=False), [b_ps[bi]], [b_mask[pc]])

            kb.stage = 'g%d.G4' % g
            kb.switch(a3_prev, a3_attn)
            a3_prev = a3_attn
            kvi = [0]
            for u in range(2):
                wq, bwq = wload(U_Q + u)
                for j in range(4):
                    h = u * 4 + j
                    bi = bank()
                    mm_fm(bi, lambda kc: wq[:, kc, j * 128:(j + 1) * 128], lambda kc: nT[:, kc, :], 8, 128, [bwq] + b_nT)
                    qk = h % 2
                    copy_op(ev_eng(), qTb[qk], psum[bi][:, :], [b_ps[bi]], [b_qTb[qk]], scale=128.0 ** -0.5)
                    kvmap = {}
                    state = {"next": 0}

                    def ensure(ku, h=h, kvmap=kvmap, state=state):
                        while state["next"] <= ku:
                            kk = kvi[0] % 3
                            kvi[0] += 1
                            n_ = state["next"]
                            kb.dma(Kbuf[kk], Kt[h][:, n_ * 1024:(n_ + 1) * 1024], b_Kbuf[kk],
                                   reads=[b_kt[2 * n_][h], b_kt[2 * n_ + 1][h]], writes=[b_Kbuf[kk]])
                            kb.dma(Vbuf[kk], Vs[n_ * 1024:(n_ + 1) * 1024, h * 128:(h + 1) * 128].rearrange("(kb p) d -> p kb d", p=128),
                                   b_Vbuf[kk], reads=b_vs[2 * n_] + b_vs[2 * n_ + 1], writes=[b_Vbuf[kk]])
                            kvmap[n_] = kk
                            state["next"] += 1

                    def k_fn(t, kvmap=kvmap, ensure=ensure):
                        ensure(t // 8)
                        return Kbuf[kvmap[t // 8]][:, (t % 8) * 128:(t % 8 + 1) * 128]

                    def v_fn(t, kvmap=kvmap):
                        return Vbuf[kvmap[t // 8]][:, t % 8, :]

                    def kv_reads(t, kvmap=kvmap):
                        return [b_Kbuf[kvmap[t // 8]], b_Vbuf[kvmap[t // 8]]]

                    def mask_fn(t):
                        return (maskT[:, t, :], b_mask[t // 8])

                    attn_core(nkb, k_fn, v_fn, mask_fn, qTb[qk], b_qTb[qk], B1[:, h, 0:512], b_B1[h], kv_reads)

            kb.stage = 'g%d.G5' % g
            for u in range(2):
                wc, bwc = wload(U_AO + u)
                wg, bwg = wload(U_GA + u)
                for j in range(4):
                    ft = u * 4 + j
                    by, bg = bank(), bank()
                    mm_fm(by, lambda kc: wc[:, kc, j * 128:(j + 1) * 128], lambda kc: B1[:, kc, 0:512], 8, 128, [bwc] + b_B1)
                    mm_fm(bg, lambda kc: wg[:, kc, j * 128:(j + 1) * 128], lambda kc: nT[:, kc, :], 8, 128, [bwg] + b_nT)
                    k = ft % 2
                    kb.op(ACT, "activation", (qTb[k], psum[bg][:, :], AF.Sigmoid), dict(bias=vecs[:, V_BGA + ft:V_BGA + ft + 1]),
                          [b_ps[bg], b_vecs], [b_qTb[k]])
                    kb.op(DVE, "tensor_tensor", (B2[:, ft, :], psum[by][:, :], qTb[k], ALU.mult), {}, [b_ps[by], b_qTb[k]], [b_B2[ft]])

            kb.stage = 'g%d.G6' % g
            for ch in range(2):
                wm, bwm = wload(U_MIX + ch)
                for tt in range(4):
                    bi = bank()
                    insts = []
                    for kc in range(8):
                        insts.append(("matmul", (psum[bi][:, :], m1[:, kc, tt * 128:(tt + 1) * 128], wm[:, kc, :]), dict(start=(kc == 0), stop=False)))
                    for kc in range(8):
                        insts.append(("matmul", (psum[bi][:, :], B2[:, kc, tt * 128:(tt + 1) * 128], wm[:, kc, :]), dict(start=False, stop=(kc == 7))))
                    kb.group(PE, insts, [bwm] + b_m1 + b_B2, [b_ps[bi]])
                    resid_add(tt, ch, bi)

            kb.stage = 'g%d.G7' % g
            for tt in range(4):
                norm_T(xt[:, tt, :], b_xt[tt], 128, nT[:, :, tt * 128:(tt + 1) * 128], [b_nT[tt]])
            wq, bwq = wload(U_XQ)
            for xhd in range(4):
                bi = bank()
                mm_fm(bi, lambda kc: wq[:, kc, xhd * 128:(xhd + 1) * 128], lambda kc: nT[:, kc, :], 8, 128, [bwq] + b_nT)
                qk = xhd % 2
                copy_op(ev_eng(), qTb[qk], psum[bi][:, :], [b_ps[bi]], [b_qTb[qk]], scale=128.0 ** -0.5)
                attn_core(2, lambda t: KmT[:, xhd, t * 128:(t + 1) * 128], lambda t: Vm[:, t, xhd * 128:(xhd + 1) * 128], None,
                          qTb[qk], b_qTb[qk], B1[:, xhd, 0:512], b_B1[xhd], lambda t: [b_KmT, b_Vm])
            for ch in range(2):
                wo, bwo = wload(U_XO + ch, nkc=4)
                for tt in range(4):
                    bi = bank()
                    insts = []
                    for kc in range(4):
                        insts.append(("matmul", (psum[bi][:, :], B1[:, kc, tt * 128:(tt + 1) * 128], wo[:, kc, :]), dict(start=(kc == 0), stop=(kc == 3))))
                    kb.group(PE, insts, [bwo] + b_B1[:4], [b_ps[bi]])
                    resid_add(tt, ch, bi)

            kb.stage = 'g%d.G8' % g
            for tt in range(4):
                norm_T(xt[:, tt, :], b_xt[tt], 128, nT[:, :, tt * 128:(tt + 1) * 128], [b_nT[tt]])
            kb.switch(a2_prev, b_aT)
            a2_prev = b_aT
            for u in range(8):
                w1, bw1 = wload(U_FF1 + u)
                for j in range(4):
                    fft = u * 4 + j
                    bi = bank()
                    mm_fm(bi, lambda kc: w1[:, kc, j * 128:(j + 1) * 128], lambda kc: nT[:, kc, :], 8, 128, [bw1] + b_nT)
                    k = fft % 6
                    kb.op(ACT, "activation", (Pb[k], psum[bi][:, :], AF.Relu), {}, [b_ps[bi]], [b_Pb[k]])
                    kb.op(mul_eng(), "tensor_tensor", (aT[:, fft, :], Pb[k], Pb[k], ALU.mult), {}, [b_Pb[k]], [b_aT[fft]])
            for ch in range(2):
                acc = [pin() for _ in range(4)]
                for kg in range(4):
                    w2, bw2 = wload(U_FF2 + ch * 4 + kg)
                    for tt in range(4):
                        insts = []
                        for kc in range(8):
                            insts.append(("matmul", (psum[acc[tt]][:, :], aT[:, kg * 8 + kc, tt * 128:(tt + 1) * 128], w2[:, kc, :]),
                                          dict(start=(kg == 0 and kc == 0), stop=(kg == 3 and kc == 7))))
                        kb.group(PE, insts, [bw2] + b_aT[kg * 8:(kg + 1) * 8], [b_ps[acc[tt]]])
                for tt in range(4):
                    resid_add(tt, ch, acc[tt])
                    unpin(acc[tt])

            kb.stage = 'g%d.G9' % g
            for tt in range(4):
                rs_ap, rs_b = norm_T(xt[:, tt, :], b_xt[tt], 128, None, None, nrm_only=True)
                kb.op(DVE, "scalar_tensor_tensor", (xt[:, tt, :], xt[:, tt, :], rs_ap, gfin[:], ALU.mult, ALU.mult), {},
                      [b_xt[tt], rs_b, b_gfin], [b_xt[tt]])
                kb.dma(out[g * 512 + tt * 128: g * 512 + (tt + 1) * 128, :], xt[:, tt, :], b_xt[tt], reads=[b_xt[tt]], writes=[b_out])

        kb.barrier()

        with nc.Block() as block:
            @block.sync
            def _(e):
                kb.replay(SP, e)

            @block.tensor
            def _(e):
                kb.replay(PE, e)

            @block.scalar
            def _(e):
                kb.replay(ACT, e)

            @block.vector
            def _(e):
                kb.replay(DVE, e)

            @block.gpsimd
            def _(e):
                kb.replay(POOL, e)

        nc._pe_labels = kb.pe_labels
        print("ops: " + ", ".join("%s=%d" % (E.name, len(E.ops)) for E in kb.engs), "sems", kb.nsem, flush=True)
    return nc


def host_inputs(inputs, NG):
    L = 1024 * NG
    f = lambda a: np.ascontiguousarray(np.asarray(a, dtype=np.float32))
    x = f(inputs["x"])[:, :L]
    B = x.shape[0]
    mem = f(inputs["mem"])

    def fm(v):
        v = f(v).reshape(-1, 128)
        return v.T

    vecs = np.concatenate([fm(inputs["norm_mix_g"][0]), fm(inputs["norm_x_g"][0]), fm(inputs["norm_mem_g"][0]),
                           fm(inputs["norm_ffn_g"][0]), fm(inputs["b_gate"][0][:1024]), fm(inputs["b_gate"][0][1024:]),
                           fm(inputs["conv_b"][0]), fm(inputs["conv_ln_g"][0]), fm(inputs["conv_ln_b"][0])], axis=1)
    cwfm = f(inputs["conv_w"][0]).T.reshape(8, 128, 31).transpose(1, 0, 2).reshape(128, 248)
    shared = dict(
        vecs=f(vecs), cwfm=f(cwfm), gfin=f(inputs["norm_final_g"]).reshape(1024),
        ident=np.eye(128, dtype=np.float32),
        pow2=f(np.tile((0.5 ** np.arange(1, NIT + 1))[None, :], (128, 1))),
        w_in=f(inputs["w_in"][0]), w_conv_out=f(inputs["w_conv_out"][0]), w_attn_out=f(inputs["w_attn_out"][0]),
        w_mix_out=f(inputs["w_mix_out"][0]), wx_q=f(inputs["wx_q"][0]), wx_kv=f(inputs["wx_kv"][0]), wx_o=f(inputs["wx_o"][0]),
        w_ff1=f(inputs["w_ff1"][0]), w_ff2=f(inputs["w_ff2"][0]),
    )
    r = np.arange(128)[:, None]
    cms = []
    for p in range(2):
        cm = np.zeros((4, 128, 1024), np.float32)
        kcol = np.arange(1024)[None, :]
        for i in range(4):
            qpos = p * 512 + i * 128 + r
            cm[i] = np.where(kcol <= qpos, 0.0, NEG)
        cms.append(cm)
    in_maps = []
    for core in range(2 * B):
        b, p = core // 2, core % 2
        xo = np.zeros((NG, 544, 1024), np.float32)
        for g in range(NG):
            c = 2 * g + p
            s = c * 512 - 32
            if s < 0:
                xo[g, 32:] = x[b, 0:512]
            else:
                xo[g] = x[b, s:s + 544]
        d = dict(shared)
        d.update(x_seq=f(x[b]), x_own=xo, mem=f(mem[b]), cmask=cms[p])
        in_maps.append(d)
    return in_maps


_NC_CACHE = {}


def kernel_impl(inputs, NG):
    if NG not in _NC_CACHE:
        _NC_CACHE[NG] = build(NG)
    nc = _NC_CACHE[NG]
    in_maps = host_inputs(inputs, NG)
    n = len(in_maps)
    res = run_bass_kernel_spmd(nc, in_maps, core_ids=list(range(n)))
    B = n // 2
    L = 1024 * NG
    o = np.zeros((B, L, 1024), np.float32)
    for core in range(n):
        b, p = core // 2, core % 2
        r = np.asarray(res.results[core]["out"], dtype=np.float32)
        for g in range(NG):
            c = 2 * g + p
            o[b, c * 512:(c + 1) * 512] = r[g * 512:(g + 1) * 512]
    return o


def kernel(**inputs):
    return kernel_impl(inputs, 8)
```
